# Optimizing a Trainium2 kernel written in Bass

```python
import jax, jax.numpy as jnp
from jax import lax
import numpy as np

D_MODEL = 1024
BATCH = 4
SEQ = 8192
DEPTH = 2

CHUNK = 64
D_MIX = D_MODEL
A_WIDTH = D_MIX // 4
A_HEADS = 4
A_HEAD_DIM = A_WIDTH // A_HEADS
A_BLOCK = 128
POOL_WINDOWS = (2, 4, 8, 16)
B_WIDTH = D_MIX // 4
B_GROUP_DIM = B_WIDTH // len(POOL_WINDOWS)
C_WIDTH = D_MIX - A_WIDTH - B_WIDTH
C_HEADS = 8
C_HEAD_DIM = C_WIDTH // C_HEADS
ROPE_BASE = 10000.0
IN_COLS = 2 * A_WIDTH + B_WIDTH + 4 * C_WIDTH
D_FF = ((8 * D_MODEL // 3 + 127) // 128) * 128
CONV_WIDTH = 3
EPS = 1e-6

kernel_name = "hybrid_gmlp_pool_retention_convffn"


def rms_norm(x, g):
    xf = x.astype(jnp.float32)
    y = xf * lax.rsqrt(jnp.mean(xf * xf, axis=-1, keepdims=True) + EPS)
    return (y * g.astype(jnp.float32)).astype(x.dtype)


def spatial_gating(z, vnorm_g, ws, bs):
    bsz, s_len, _ = z.shape
    u, v = jnp.split(z, 2, axis=-1)
    v = rms_norm(v, vnorm_g)
    nb = s_len // A_BLOCK
    v = v.reshape(bsz, nb, A_BLOCK, A_HEADS, A_HEAD_DIM)
    chunk_id = jnp.arange(A_BLOCK) // CHUNK
    mask = chunk_id[:, None] >= chunk_id[None, :]
    w = jnp.where(mask[None], ws, jnp.zeros_like(ws))
    sv = jnp.einsum('hts,bnshd->bnthd', w, v) + bs.T[None, None, :, :, None]
    return u * sv.reshape(bsz, s_len, A_WIDTH)


def multiscale_pool(xb, w_grp, scale):
    bsz, s_len, _ = xb.shape
    xf = xb.astype(jnp.float32)
    cs = jnp.concatenate([jnp.zeros((bsz, 1, B_WIDTH), jnp.float32), jnp.cumsum(xf, axis=1)], axis=1)
    t = jnp.arange(s_len)
    outs = []
    for gi, win in enumerate(POOL_WINDOWS):
        sl = slice(gi * B_GROUP_DIM, (gi + 1) * B_GROUP_DIM)
        lo = jnp.maximum(t + 1 - win, 0)
        cnt = (t + 1 - lo).astype(jnp.float32)
        outs.append((cs[:, 1:, sl] - cs[:, lo, sl]) / cnt[None, :, None])
    pooled = jnp.concatenate(outs, axis=-1).astype(xb.dtype) - xb
    pooled = pooled.reshape(bsz, s_len, len(POOL_WINDOWS), B_GROUP_DIM)
    y = jnp.einsum('bsgc,gcd->bsgd', pooled, w_grp).reshape(bsz, s_len, B_WIDTH)
    return y * scale


def rotary(x, pos):
    half = x.shape[-1] // 2
    inv = ROPE_BASE ** (-jnp.arange(half, dtype=jnp.float32) / half)
    ang = pos.astype(jnp.float32)[:, None] * inv[None, :]
    cos = jnp.cos(ang)[None, :, None, :]
    sin = jnp.sin(ang)[None, :, None, :]
    xf = x.astype(jnp.float32)
    x1, x2 = xf[..., :half], xf[..., half:]
    return jnp.concatenate([x1 * cos - x2 * sin, x2 * cos + x1 * sin], axis=-1).astype(x.dtype)


def retention(q, k, v, g, norm_g):
    bsz, s_len, _ = q.shape
    n_chunks = s_len // CHUNK
    dt = q.dtype
    pos = jnp.arange(s_len)
    q = rotary(q.reshape(bsz, s_len, C_HEADS, C_HEAD_DIM), pos) * (C_HEAD_DIM ** -0.5)
    k = rotary(k.reshape(bsz, s_len, C_HEADS, C_HEAD_DIM), pos)
    v = v.reshape(bsz, s_len, C_HEADS, C_HEAD_DIM)
    log_gamma = jnp.log1p(-jnp.exp2(-5.0 - jnp.arange(C_HEADS, dtype=jnp.float32)))
    idx = jnp.arange(CHUNK, dtype=jnp.float32)
    d_intra = jnp.exp(log_gamma[:, None, None] * jnp.abs(idx[:, None] - idx[None, :])).astype(dt)
    k_dec = jnp.exp(log_gamma[None, :] * (CHUNK - 1 - idx)[:, None]).astype(dt)
    q_dec = jnp.exp(log_gamma[None, :] * (idx + 1)[:, None]).astype(dt)
    chunk_dec = jnp.exp(log_gamma * CHUNK).astype(dt)
    qc = q.reshape(bsz, n_chunks, CHUNK, C_HEADS, C_HEAD_DIM)
    kc = k.reshape(bsz, n_chunks, CHUNK, C_HEADS, C_HEAD_DIM)
    vc = v.reshape(bsz, n_chunks, CHUNK, C_HEADS, C_HEAD_DIM)
    scores = jnp.einsum('bnthd,bnshd->bnhts', qc, kc) * d_intra
    y_intra = jnp.einsum('bnhts,bnshe->bnthe', scores, vc)
    kv = jnp.einsum('bnshd,bnshe->nbhde', kc * k_dec[:, :, None], vc)

    def step(state, kv_n):
        return state * chunk_dec[None, :, None, None] + kv_n, state

    _, s_prev = lax.scan(step, jnp.zeros(kv.shape[1:], kv.dtype), kv)
    y_cross = jnp.einsum('bnthd,nbhde->bnthe', qc * q_dec[:, :, None], s_prev)
    y = (y_intra + y_cross).reshape(bsz, s_len, C_HEADS, C_HEAD_DIM)
    yf = y.astype(jnp.float32)
    mu = jnp.mean(yf, axis=-1, keepdims=True)
    var = jnp.mean(jnp.square(yf - mu), axis=-1, keepdims=True)
    yf = (yf - mu) * lax.rsqrt(var + EPS)
    y = (yf.reshape(bsz, s_len, C_WIDTH) * norm_g.astype(jnp.float32)).astype(dt)
    return jax.nn.silu(g) * y


def conv_ffn(h, w_up, conv_w, conv_b, w_down):
    up = h @ w_up
    s_len = up.shape[1]
    padded = jnp.pad(up, ((0, 0), (CONV_WIDTH - 1, 0), (0, 0)))
    conv = conv_b + padded[:, 0:s_len] * conv_w[0]
    for j in range(1, CONV_WIDTH):
        conv = conv + padded[:, j:j + s_len] * conv_w[j]
    gate, val = jnp.split(conv, 2, axis=-1)
    return (jax.nn.silu(gate) * val) @ w_down


def setup_inputs(seed: int = 0) -> dict:
    key = jax.random.key(seed)
    ks = jax.random.split(key, 20)
    f32 = jnp.float32
    nrm = lambda k, shape, s: jax.random.normal(k, shape, f32) * s
    return {
        "x": jax.random.normal(ks[0], (BATCH, SEQ, D_MODEL), f32),
        "norm1_g": 1.0 + nrm(ks[1], (DEPTH, D_MODEL), 0.05),
        "w_in": nrm(ks[2], (DEPTH, D_MODEL, IN_COLS), D_MODEL ** -0.5),
        "a_vnorm_g": 1.0 + nrm(ks[3], (DEPTH, A_WIDTH), 0.05),
        "a_ws": nrm(ks[4], (DEPTH, A_HEADS, A_BLOCK, A_BLOCK), 0.5 * A_BLOCK ** -0.5),
        "a_bs": 1.0 + nrm(ks[5], (DEPTH, A_HEADS, A_BLOCK), 0.1),
        "b_w": nrm(ks[6], (DEPTH, len(POOL_WINDOWS), B_GROUP_DIM, B_GROUP_DIM), B_GROUP_DIM ** -0.5),
        "b_scale": 1.0 + nrm(ks[7], (DEPTH, B_WIDTH), 0.1),
        "c_norm_g": 1.0 + nrm(ks[8], (DEPTH, C_WIDTH), 0.05),
        "w_out": nrm(ks[9], (DEPTH, D_MIX, D_MODEL), D_MIX ** -0.5),
        "norm2_g": 1.0 + nrm(ks[10], (DEPTH, D_MODEL), 0.05),
        "w_up": nrm(ks[11], (DEPTH, D_MODEL, 2 * D_FF), D_MODEL ** -0.5),
        "conv_w": nrm(ks[12], (DEPTH, CONV_WIDTH, 2 * D_FF), CONV_WIDTH ** -0.5),
        "conv_b": nrm(ks[13], (DEPTH, 2 * D_FF), 0.02),
        "w_down": nrm(ks[14], (DEPTH, D_FF, D_MODEL), D_FF ** -0.5),
        "final_g": 1.0 + nrm(ks[15], (D_MODEL,), 0.05),
    }


def reference(x, norm1_g, w_in, a_vnorm_g, a_ws, a_bs, b_w, b_scale, c_norm_g,
              w_out, norm2_g, w_up, conv_w, conv_b, w_down, final_g):
    splits = [2 * A_WIDTH, 2 * A_WIDTH + B_WIDTH, 2 * A_WIDTH + B_WIDTH + C_WIDTH,
              2 * A_WIDTH + B_WIDTH + 2 * C_WIDTH, 2 * A_WIDTH + B_WIDTH + 3 * C_WIDTH]
    for l in range(DEPTH):
        h = rms_norm(x, norm1_g[l])
        proj = h @ w_in[l]
        za, xb, q, k, v, g = jnp.split(proj, splits, axis=-1)
        ya = spatial_gating(jax.nn.gelu(za), a_vnorm_g[l], a_ws[l], a_bs[l])
        yb = multiscale_pool(xb, b_w[l], b_scale[l])
        yc = retention(q, k, v, g, c_norm_g[l])
        x = x + jnp.concatenate([ya, yb, yc], axis=-1) @ w_out[l]
        x = x + conv_ffn(rms_norm(x, norm2_g[l]), w_up[l], conv_w[l], conv_b[l], w_down[l])
    return rms_norm(x, final_g)
```

```python
import contextlib
import numpy as np
import concourse.bass as bass
import concourse.mybir as mybir
from concourse.bass_utils import run_bass_kernel_spmd

F32 = mybir.dt.float32
BF16 = mybir.dt.bfloat16
AF = mybir.ActivationFunctionType
ALU = mybir.AluOpType
AX = mybir.AxisListType

D = 1024
DFF = 2816
NJ = 22
EPS = 1e-6
TILE = 512
NRING = 5
import os as _os
PENW = float(_os.environ.get("KPENW", "1.0"))
ELIG = float(_os.environ.get("KELIG", "0.2"))
SLOT = 4096


class _Op:
    __slots__ = ("eng", "fn", "deps", "dma", "chan", "sig", "idx", "tok", "inc")

    def __init__(self):
        self.tok = None


class _Rec:
    def __getattr__(self, name):
        def f(*a, **k):
            self.__dict__["call"] = (name, a, k)
            return self
        return f


class Prog:
    COMPUTE = ("pe", "act", "dve", "pool")

    def __init__(self, nc):
        self.nc = nc
        self.ops = []
        self.last_write = {}
        self.readers = {}
        self.section = None
        import os
        self.skip = set(filter(None, os.environ.get("KSKIP", "").split(",")))

    def add(self, eng, fn, reads=(), writes=(), dma=False, chan=None, sig=True, inc=16):
        if self.section in self.skip:
            return None
        op = _Op()
        rec = _Rec()
        fn(rec)
        op.eng, op.fn, op.dma, op.chan, op.sig = eng, rec.call, dma, chan, sig
        op.inc = inc
        op.idx = len(self.ops)
        pr = [r for r in reads if isinstance(r, tuple) and r[0] in ("psF", "psT")]
        if pr:
            reads = [r for r in reads if r not in pr]
            writes = list(writes) + pr
        deps = {}
        for r in reads:
            w = self.last_write.get(r)
            if w is not None:
                deps[w.idx] = (w, True)
        for r in writes:
            w = self.last_write.get(r)
            if w is not None and w.idx not in deps:
                deps[w.idx] = (w, False)
            for rd in self.readers.get(r, ()):
                if rd.idx not in deps and rd is not op:
                    deps[rd.idx] = (rd, False)
        op.deps = list(deps.values())
        for r in reads:
            self.readers.setdefault(r, []).append(op)
        for r in writes:
            self.last_write[r] = op
            self.readers[r] = []
        self.ops.append(op)
        return op

    def pe(self, fn, reads=(), writes=(), sig=True):
        return self.add("pe", fn, reads, writes, sig=sig)

    def act(self, fn, reads=(), writes=()):
        return self.add("act", fn, reads, writes)

    def dve(self, fn, reads=(), writes=()):
        return self.add("dve", fn, reads, writes)

    def pool(self, fn, reads=(), writes=()):
        return self.add("pool", fn, reads, writes)

    def dma(self, queue, out, in_, reads=(), writes=(), chan=None, **kw):
        assert chan is not None
        return self.add(queue, lambda e: e.dma_start(out=out, in_=in_, **kw),
                        reads, list(writes) + [("chan", chan)], dma=True, chan=chan)

    def async_op(self, queue, fn, reads=(), writes=(), chan=None, inc=1):
        return self.add(queue, fn, reads, list(writes) + [("chan", chan)], dma=True, chan=chan, inc=inc)

    @staticmethod
    def _free(ap):
        n = 1
        for d in ap.shape[1:]:
            n *= d
        return n

    def _dur(self, op):
        name, a, k = op.fn
        if op.dma:
            if name != "dma_start":
                return 0.7, 90.0
            o = k["out"]
            nbytes = o.shape[0] * self._free(o) * (2 if o.dtype == BF16 else 4)
            return (0.65 if op.eng == "pool" else 0.15), 2.0 + nbytes / 160e3
        if op.eng == "pe":
            if name == "transpose":
                return 0.1, 0.1
            n = self._free(k["rhs"])
            d = max(n / 2400.0 + 0.005, 0.11)
            return d, d
        out = k.get("out", k.get("ap", None))
        n = self._free(out) if out is not None else 64
        if op.eng == "act":
            d = 0.22 + n * 0.9e-3
        elif op.eng == "dve":
            d = 0.08 + n * 1.25e-3
        else:
            d = 0.3 + n * 2.2e-3
        return d, d

    def schedule(self, window=32):
        ops = self.ops
        units = []
        unit_of = {}
        cur = None
        for op in ops:
            if op.eng == "pe" and not op.dma:
                if cur is None:
                    cur = [op.eng, [], set(), 0.0, 0.0, op.idx]
                    units.append(cur)
                cur[1].append(op)
                unit_of[op.idx] = len(units) - 1
                b, _ = self._dur(op)
                cur[3] += b
                cur[4] = cur[3]
                if op.sig:
                    cur = None
            else:
                b, lat = self._dur(op)
                units.append([op.eng, [op], set(), b, lat, op.idx])
                unit_of[op.idx] = len(units) - 1
        assert cur is None
        for ui, u in enumerate(units):
            for op in u[1]:
                for (d, raw) in op.deps:
                    du = unit_of[d.idx]
                    if du != ui:
                        u[2].add(du)
        blev = [0.0] * len(units)
        for ui in range(len(units) - 1, -1, -1):
            u = units[ui]
            mine = blev[ui] + u[4]
            for d in u[2]:
                if mine > blev[d]:
                    blev[d] = mine
        pend = {}
        for ui, u in enumerate(units):
            pend.setdefault(u[0], []).append(ui)
        ptr = {e: 0 for e in pend}
        TSET = {str(AF.Sqrt): "sqrt", str(AF.Silu): "silu", str(AF.Gelu_apprx_tanh): "gelu"}
        act_set = [None]

        def tset(ui):
            u = units[ui]
            if u[0] != "act":
                return None
            return TSET.get(str(u[1][0].fn[2].get("func", "")))
        done = {}
        free_at = {e: 0.0 for e in pend}
        flex = ("pe", "act", "dve")
        order = {e: [] for e in pend}
        taken = [False] * len(units)
        nleft = len(units)
        SEM = 0.25

        def ready_time(ui):
            u = units[ui]
            t = 0.0
            for d in u[2]:
                f = done.get(d)
                if f is None:
                    return None
                if units[d][0] != u[0]:
                    f += SEM
                if f > t:
                    t = f
            return t

        while nleft:
            best = None
            for e, lst in pend.items():
                p = ptr[e]
                while p < len(lst) and taken[lst[p]]:
                    p += 1
                ptr[e] = p
                if p >= len(lst):
                    continue
                cand = None
                lim = window if e in flex else 1
                seen = 0
                q = p
                cl = []
                while q < len(lst) and seen < lim:
                    ui = lst[q]
                    q += 1
                    if taken[ui]:
                        continue
                    seen += 1
                    r = ready_time(ui)
                    if r is None:
                        continue
                    st = max(r, free_at[e])
                    pen = 0.0
                    if e == "act":
                        ts = tset(ui)
                        if ts is not None and ts != act_set[0]:
                            pen = 1.3
                    cl.append((st, ui, pen))
                if cl:
                    mst = min(c[0] for c in cl)
                    c = max((c for c in cl if c[0] <= mst + ELIG), key=lambda c: (blev[c[1]] - PENW * c[2], -c[1]))
                    cand = (c[0] + c[2], c[1])
                if cand is not None and (best is None or cand[0] < best[0]):
                    best = (cand[0], cand[1], e)
            assert best is not None, "scheduler deadlock"
            st, ui, e = best
            u = units[ui]
            taken[ui] = True
            nleft -= 1
            if e == "act":
                ts = tset(ui)
                if ts is not None:
                    act_set[0] = ts
            free_at[e] = st + u[3]
            done[ui] = st + u[4]
            order[e].append(ui)
            if getattr(self, "timeline", None) is not None:
                self.timeline.append((e, st, u[3], u[1][0].fn[0], u[5], ui))
        self.sched_makespan = max(done.values())
        self._units, self._done, self._unit_of = units, done, unit_of
        eng_ops = {}
        pos = {}
        n = 0
        for e, lst in order.items():
            eng_ops[e] = []
            for ui in lst:
                for op in units[ui][1]:
                    eng_ops[e].append(op)
        return eng_ops

    def finalize(self, final_wait_chans=(), schedule=True):
        nc = self.nc
        ops = self.ops
        if schedule:
            eng_ops = self.schedule()
        else:
            eng_ops = {}
            for op in ops:
                eng_ops.setdefault(op.eng, []).append(op)
        spos = {}
        for e, lst in eng_ops.items():
            for i, op in enumerate(lst):
                spos[op.idx] = i
        chans = []
        seen_c = set()
        for op in ops:
            if op.dma and op.chan not in seen_c:
                seen_c.add(op.chan)
                chans.append(op.chan)
        sem_keys = [("e", e) for e in self.COMPUTE] + [("c", c) for c in chans]
        stack = contextlib.ExitStack()
        sems = {}
        for i, key in enumerate(sem_keys):
            sems[key] = stack.enter_context(nc.semaphore("s%d" % i))
        for e in self.COMPUTE:
            cnt = 0
            pend = []
            for o in eng_ops.get(e, []):
                if o.dma:
                    continue
                pend.append(o)
                if o.sig:
                    cnt += 1
                    for p in pend:
                        p.tok = (("e", e), cnt, o.idx)
                    pend = []
            assert not pend, "trailing non-sig ops on %s" % e
        ccount = {}
        for op in ops:
            if op.dma:
                ccount[op.chan] = ccount.get(op.chan, 0) + op.inc
                op.tok = (("c", op.chan), ccount[op.chan], op.idx)

        def emit_stream(e, handle):
            seen = {}
            for op in eng_ops.get(e, []):
                for (d, raw) in op.deps:
                    if (not d.dma) and (not op.dma) and d.eng == e and e == "pe":
                        continue
                    key, val, sidx = d.tok
                    if seen.get(key, 0) >= val:
                        continue
                    seen[key] = val
                    handle.wait_ge(sems[key], val)
                name, a, k = op.fn
                ins = getattr(handle, name)(*a, **k)
                if op.dma:
                    ins.then_inc(sems[op.tok[0]], op.inc)
                elif op.sig:
                    ins.then_inc(sems[("e", e)], 1)
            if e == "sp":
                for c in final_wait_chans:
                    handle.wait_ge(sems[("c", c)], ccount[c])

        with nc.Block() as block:
            @block.sync
            def _(h):
                emit_stream("sp", h)

            @block.tensor
            def _(h):
                emit_stream("pe", h)

            @block.scalar
            def _(h):
                emit_stream("act", h)

            @block.vector
            def _(h):
                emit_stream("dve", h)

            @block.gpsimd
            def _(h):
                emit_stream("pool", h)
        stack.close()


def _chunk_table():
    t = []
    off = 0
    for gi in range(6):
        w = 4096 if gi < 5 else 2048
        t.append(("win%d" % gi, "W", off, w, 128))
        off += w
    t.append(("wouta0", "W", off, 4096, 128)); off += 4096
    t.append(("wouta1", "W", off, 2048, 128)); off += 2048
    t.append(("woutb", "WB", 0, 2048, 128))
    for c in range(11):
        t.append(("wup%d" % c, "W", off, 4096, 128)); off += 4096
    for n in range(2):
        for c in range(3):
            w = 4096 if c < 2 else 6 * 512
            t.append(("wdn%d_%d" % (n, c), "W", off, w, 128)); off += w
    return t, off


CHUNKS, WCOLS = _chunk_table()


def _prep_layer(w_in, w_out, w_up, w_down):
    parts = []
    wi = w_in.reshape(8, 128, 2816).transpose(1, 0, 2)
    groups = [(0, 512), (768, 1280), (1280, 1792), (1792, 2304), (2304, 2816), (512, 768)]
    for (a, b) in groups:
        parts.append(wi[:, :, a:b].reshape(128, -1))
    wa = np.concatenate([w_out[0:256], w_out[512:1024]], axis=0).reshape(6, 128, 1024).transpose(1, 0, 2)
    parts.append(wa[:, 0:4].reshape(128, -1))
    parts.append(wa[:, 4:6].reshape(128, -1))
    wb = w_out[256:512].reshape(2, 2, 64, 1024).transpose(1, 2, 0, 3).reshape(128, 2048)
    wu = w_up.reshape(8, 128, 2, NJ, 128).transpose(1, 3, 0, 2, 4)
    for c in range(11):
        parts.append(wu[:, 2 * c:2 * c + 2].reshape(128, -1))
    wd = w_down.reshape(NJ, 128, 2, 512).transpose(1, 2, 0, 3)
    for n in range(2):
        for (a, b) in ((0, 8), (8, 16), (16, 22)):
            parts.append(wd[:, n, a:b].reshape(128, -1))
    W = np.ascontiguousarray(np.concatenate(parts, axis=1), dtype=np.float32)
    assert W.shape == (128, WCOLS)
    return W, np.ascontiguousarray(wb, dtype=np.float32)


SP_G1T, SP_G2T, SP_GV, SP_NG, SP_BST, SP_CW, SP_BSC = 0, 8, 16, 272, 784, 788, 964
SPW = 968


def _prep_small(norm1_g, norm2_g, a_vnorm_g, c_norm_g, a_bs, conv_w, conv_b, b_scale):
    sp = np.zeros((128, SPW), np.float32)
    sp[:, SP_G1T:SP_G1T + 8] = norm1_g.reshape(8, 128).T
    sp[:, SP_G2T:SP_G2T + 8] = norm2_g.reshape(8, 128).T
    sp[:, SP_GV:SP_GV + 256] = a_vnorm_g[None, :]
    sp[:, SP_NG:SP_NG + 512] = c_norm_g[None, :]
    sp[:, SP_BST:SP_BST + 4] = a_bs.T
    cw = np.concatenate([conv_w, conv_b[None]], axis=0)
    sp[:, SP_CW:SP_CW + 176] = cw.reshape(4, 44, 128).transpose(2, 1, 0).reshape(128, 176)
    sp[:, SP_BSC:SP_BSC + 2] = b_scale.reshape(2, 2, 64).transpose(1, 2, 0).reshape(128, 2)
    return sp


def _constants(n_tiles):
    H, C = 8, 64
    lg = np.log1p(-np.exp2(-5.0 - np.arange(H, dtype=np.float64)))
    t = np.arange(128)
    same = (t[:, None] // 64) == (t[None, :] // 64)
    dm = np.exp(lg[None, :, None] * np.abs(t[:, None, None] - t[None, None, :])) * same[:, None, :] * 0.125
    qd = np.zeros((128, 4, 128))
    cd = np.zeros((128, 4))
    for r in range(2):
        for j in range(4):
            h = 2 * j + r
            qd[64 * r:64 * r + 64, j, :] = np.exp(lg[h] * ((t % 64) + 1))[None, :] * 0.125
            cd[64 * r:64 * r + 64, j] = np.exp(lg[h] * 64)
    kd = np.exp(lg[None, :] * (63 - (t % 64))[:, None])
    wins = (2, 4, 8, 16)
    pm = np.zeros((128, 3, 4, 128))
    for g, w in enumerate(wins):
        for tt in range(128):
            for ss in range(max(0, tt - w + 1), tt + 1):
                pm[ss, 0, g, tt] += 1.0 / w
            for ss in range(128 + tt - w + 1, 128):
                pm[ss, 1, g, tt] += 1.0 / w
            cnt = min(tt + 1, w)
            for ss in range(max(0, tt - w + 1), tt + 1):
                pm[ss, 2, g, tt] += 1.0 / cnt
            pm[tt, 0, g, tt] -= 1.0
            pm[tt, 2, g, tt] -= 1.0
    amask = ((t[None, :] // 64) >= (t[:, None] // 64)).astype(np.float64)
    cst = np.concatenate([dm.reshape(128, -1), qd.reshape(128, -1), cd, kd, amask], axis=1).astype(np.float32)
    cbf = np.concatenate([pm.reshape(128, -1), np.eye(128), pm[:, 2].reshape(128, -1), pm[:, 0].reshape(128, -1)],
                         axis=1).astype(np.float32)
    half = 32
    inv = 10000.0 ** (-np.arange(half, dtype=np.float64) / half)
    pos = np.arange(n_tiles * TILE, dtype=np.float64)
    ang = (pos.astype(np.float32)[:, None] * inv.astype(np.float32)[None, :]).astype(np.float32)
    cs = np.stack([np.cos(ang), np.sin(ang)], axis=1).astype(np.float32)
    rope = cs.reshape(n_tiles, 4, 128, 2, 32).transpose(0, 2, 3, 1, 4)
    return cst, cbf, np.ascontiguousarray(rope, dtype=np.float32)


CST_DM, CST_QD, CST_CD, CST_KD, CST_AM, CSTW = 0, 1024, 1536, 1540, 1548, 1676
CBF_PM, CBF_ID, CBF_PS0, CBF_PSL, CBFW = 0, 1536, 1664, 2176, 2688
GROUPS = [[0, 1], [2, 3], [4, 5], [6, 7]]


def build_program(n_tiles, layers, final_norm=True, lag=0, pipe=False):
    nc = bass.Bass("TRN2", target_bir_lowering=False)
    L = len(layers)
    ntok = n_tiles * TILE
    ROLE_d = nc.dram_tensor("ROLE", [128, 4], F32, kind="ExternalInput").ap()
    if pipe:
        stage_d = [nc.dram_tensor("stage%d" % i, [TILE, D], F32, kind="Internal").ap() for i in range(2)]
        gath_d = [nc.dram_tensor("gath%d" % i, [2 * TILE, D], F32, kind="Internal").ap() for i in range(2)]
    x_d = nc.dram_tensor("x", [ntok, D], F32, kind="ExternalInput").ap()
    out_d = nc.dram_tensor("out", [ntok, D], F32, kind="ExternalOutput").ap()
    W_d = [nc.dram_tensor("W%d" % l, [128, WCOLS], F32, kind="ExternalInput").ap() for l in range(L)]
    WB_d = [nc.dram_tensor("WB%d" % l, [128, 2048], F32, kind="ExternalInput").ap() for l in range(L)]
    SP_d = [nc.dram_tensor("SP%d" % l, [128, SPW], F32, kind="ExternalInput").ap() for l in range(L)]
    WS_d = [nc.dram_tensor("WS%d" % l, [128, 512], F32, kind="ExternalInput").ap() for l in range(L)]
    BW_d = [nc.dram_tensor("BW%d" % l, [64, 256], F32, kind="ExternalInput").ap() for l in range(L)]
    GF_d = nc.dram_tensor("GF", [128, D], F32, kind="ExternalInput").ap()
    CST_d = nc.dram_tensor("CST", [128, CSTW], F32, kind="ExternalInput").ap()
    CBF_d = nc.dram_tensor("CBF", [128, CBFW], F32, kind="ExternalInput").ap()
    ROPE_d = nc.dram_tensor("ROPE", [n_tiles, 128, 256], F32, kind="ExternalInput").ap()
    scr = {}
    for l in range(L):
        for (name, src, off, w, parts) in CHUNKS:
            scr[(l, name)] = nc.dram_tensor("scr_%d_%s" % (l, name), [parts, w], BF16, kind="Internal").ap()

    A = nc.alloc_sbuf_tensor
    xsb = [A("xs%d" % i, [128, 4, D], F32) for i in range(2)]
    cx = {"p": 0}

    def xs_():
        return xsb[cx["p"]]

    def xr(i):
        return ("x", cx["p"], i)
    ob = [A("ob%d" % i, [128, D], F32) for i in range(2)]
    hT = A("hT", [128, 8, TILE], BF16)
    hn = [A("hn%d" % i, [128, D], BF16) for i in range(2)]
    junks = [A("junk%d" % i, [128, D], BF16) for i in range(2)]
    jc = {"n": 0}

    def junk_():
        jc["n"] += 1
        k = jc["n"] % 2
        return junks[k], ("junk", k)
    st = A("st", [128, 64], F32)
    ring = [A("ring%d" % i, [128, SLOT], BF16) for i in range(NRING)]
    us = A("us", [128, 4, 256], F32)
    va = A("va", [128, 4, 256], BF16)
    Fs = [A("F%d" % i, [128, 512], F32) for i in range(6)]
    big = A("big", [128, NJ * TILE], BF16)
    actT = big[:, :].rearrange("p (j t) -> p j t", j=NJ)
    qrot = big[:, 0:2048].rearrange("p (i c) -> p i c", i=4)
    krot = big[:, 2048:4096].rearrange("p (i c) -> p i c", i=4)
    kd = big[:, 4096:6144].rearrange("p (i c) -> p i c", i=4)
    vs = big[:, 6144:8192].rearrange("p (i c) -> p i c", i=4)
    gz = big[:, 8192:10240].rearrange("p (i c) -> p i c", i=4)
    xb = [A("xb%d" % l, [128, 5, 256], BF16) for l in range(L)]
    qTs = [A("qT%d" % i, [128, 4, 128], BF16) for i in range(2)]
    qdTs = [A("qdT%d" % i, [128, 4, 128], BF16) for i in range(2)]
    kTs = [A("kT%d" % i, [128, 4, 128], BF16) for i in range(2)]
    scTs = [A("scT%d" % i, [128, 8, 128], BF16) for i in range(2)]
    S = [A("S%d" % l, [128, 4, 64], F32) for l in range(L)]
    Stmp = A("Stmp", [128, 4, 64], F32)
    Scar = [A("Scar%d" % l, [128, 8, 64], BF16) for l in range(L)]
    Sring = A("Sring", [128, 8, 8, 64], BF16)
    ycbs = [A("ycb%d" % i, [128, 512], BF16) for i in range(2)]
    yab = A("yab", [128, 256], BF16)
    ycT = A("ycT", [128, 4, TILE], BF16)
    yaT = A("yaT", [128, 2, TILE], BF16)
    ybT = A("ybT", [128, 2, TILE], BF16)
    plb = A("plb", [64, 512], BF16)
    halo = [A("halo%d" % l, [128, 44, 2], F32) for l in range(L)]
    hc = A("hc", [128, 44, 2], F32)
    hct = A("hct", [128, 44], F32)
    spm = [A("spm%d" % l, [128, SPW], F32) for l in range(L)]
    wmT = [A("wmT%d" % l, [128, 4, 128], BF16) for l in range(L)]
    wsf = A("wsf", [128, 4, 128], F32)
    bw = [A("bw%d" % l, [64, 4, 64], BF16) for l in range(L)]
    gf = A("gf", [128, D], F32)
    cst = A("cst", [128, CSTW], F32)
    cbf = A("cbf", [128, CBFW], BF16)
    rope = [A("rope%d" % i, [128, 2, 4, 32], F32) for i in range(2)]
    role = A("role", [128, 4], F32)
    xg = [A("xg%d" % i, [128, D], F32) for i in range(2)] if pipe else None

    psT = [nc.alloc_psum_tensor("psT%d" % i, [128, 1024], BF16) for i in range(2)]
    psF = [nc.alloc_psum_tensor("psF%d" % i, [128, 512], F32) for i in range(6)]

    P = Prog(nc)
    cnt = {"psF": 0, "psT": 0, "x": 0, "ob": 0, "hn": 0, "qk": 0, "st": 0, "cg": 0}

    def bank():
        i = cnt["psF"] % 6
        cnt["psF"] += 1
        return psF[i], ("psF", i)

    def tbank():
        i = cnt["psT"] % 2
        cnt["psT"] += 1
        return psT[i], ("psT", i)

    def stcol(n=1):
        c = cnt["st"]
        if c + n > 64:
            c = 0
        cnt["st"] = c + n
        return st[:, c:c + n], [("st", k) for k in range(c, c + n)]

    dm_v = cst[:, CST_DM:CST_DM + 1024].rearrange("p (h t) -> p h t", h=8)
    qd_v = cst[:, CST_QD:CST_QD + 512].rearrange("p (j t) -> p j t", j=4)
    cd_v = cst[:, CST_CD:CST_CD + 4]
    kd_v = cst[:, CST_KD:CST_KD + 8]
    am_v = cst[:, CST_AM:CST_AM + 128]
    pm_v = cbf[:, CBF_PM:CBF_PM + 1536].rearrange("p (a g t) -> p a g t", a=3, g=4)
    ps0_v = cbf[:, CBF_PS0:CBF_PS0 + 512].rearrange("p (g t) -> p g t", g=4)
    psl_v = cbf[:, CBF_PSL:CBF_PSL + 512].rearrange("p (g t) -> p g t", g=4)
    ident = cbf[:, CBF_ID:CBF_ID + 128]

    P.dma("sp", cst[:], CST_d, writes=["cst"], chan="cst")
    P.dma("pool", cbf[:], CBF_d, writes=["cbf"], chan="cbf")
    P.dma("sp", gf[:], GF_d, writes=["gf"], chan="gf")
    P.dma("sp", role[:], ROLE_d, writes=["role"], chan="role")
    for l in range(L):
        P.dma("sp", spm[l][:], SP_d[l], writes=[("spm", l)], chan=("spm", l))
        P.dma("pool", bw[l][:], BW_d[l].rearrange("p (g d) -> p g d", g=4), writes=[("bw", l)], chan=("bw", l))
        P.dma("sp", wsf[:], WS_d[l].rearrange("p (h t) -> p h t", h=4), writes=["wsf"], chan="wsf")
        P.dve(lambda e, l=l: e.tensor_tensor(out=wmT[l][:], in0=wsf[:],
                                             in1=am_v.unsqueeze(1).to_broadcast([128, 4, 128]), op=ALU.mult),
              reads=["wsf", "cst"], writes=[("wmT", l)])
        P.dve(lambda e, l=l: e.memset(S[l][:], 0.0), writes=[("S", l)])
        P.dve(lambda e, l=l: e.memset(Scar[l][:], 0.0), writes=[("Scar", l)])
        P.dve(lambda e, l=l: e.memset(halo[l][:], 0.0), writes=[("halo", l)])
        P.dve(lambda e, l=l: e.memset(xb[l][:, 0, :], 0.0), writes=[("xb", l, 0)])
    P.dve(lambda e: e.memset(Sring[:], 0.0), writes=[("Sring", n) for n in range(8)])

    seq = []
    for t in range(n_tiles):
        for l in range(L):
            for (name, src, off, w, parts) in CHUNKS:
                seq.append((l, name, w, parts))
    wst = {"issued": 0, "used": 0, "released": 0}

    def pump():
        sec, P.section = P.section, None
        while wst["issued"] < min(wst["released"] + NRING, len(seq)):
            i = wst["issued"]
            l, name, w, parts = seq[i]
            s = i % NRING
            if i < L * len(CHUNKS):
                cname, csrc, coff, cw_, cparts = CHUNKS[i % len(CHUNKS)]
                assert cname == name
                srcap = W_d[l][:, coff:coff + w] if csrc == "W" else WB_d[l][:, :]
                P.dma("pool", ring[s][0:parts, 0:w], srcap, writes=[("ring", s)], chan=("ringc", s))
                P.dma("sp", scr[(l, name)], ring[s][0:parts, 0:w], reads=[("ring", s)],
                      writes=[("scr", l, name)], chan=("scrw", i % 8))
            else:
                P.dma("sp", ring[s][0:parts, 0:w], scr[(l, name)], reads=[("scr", l, name)],
                      writes=[("ring", s)], chan=("ring", s))
            wst["issued"] += 1
        P.section = sec

    def use_chunk(l, name):
        i = wst["used"]
        assert seq[i][0] == l and seq[i][1] == name, (seq[i], l, name)
        pump()
        assert wst["issued"] > i
        wst["used"] += 1
        s = i % NRING
        return ring[s], ("ring", s)

    def release(n=1):
        wst["released"] += n
        pump()

    def rstd_from_ss(ss_ap, ss_res, n, scale):
        P.dve(lambda e: e.tensor_scalar(out=ss_ap, in0=ss_ap, scalar1=scale, scalar2=EPS,
                                        op0=ALU.mult, op1=ALU.add), reads=ss_res, writes=ss_res)
        P.dve(lambda e: e.reciprocal(out=ss_ap, in_=ss_ap), reads=ss_res, writes=ss_res)
        P.act(lambda e: e.activation(out=ss_ap, in_=ss_ap, func=AF.Sqrt), reads=ss_res, writes=ss_res)

    def norm_T(l, i, gcol):
        xi = xs_()[:, i, :]
        ss, ssr = stcol()
        P.dve(lambda e: e.memset(ss, 0.0), writes=ssr)
        jk, jkr = junk_()
        P.act(lambda e: e.activation(out=jk[:], in_=xi, func=AF.Square, accum_out=ss),
              reads=[xr(i)] + ssr, writes=[jkr] + ssr)
        rstd_from_ss(ss, ssr, 1, 1.0 / D)
        hb = cnt["hn"] % 2
        cnt["hn"] += 1
        P.act(lambda e: e.activation(out=hn[hb][:], in_=xi, func=AF.Copy, scale=ss),
              reads=[xr(i)] + ssr, writes=[("hn", hb)])
        pt, ptr = tbank()
        for k in range(8):
            P.pe(lambda e, k=k: e.transpose(out=pt[:, k * 128:(k + 1) * 128], in_=hn[hb][:, k * 128:(k + 1) * 128],
                                            identity=ident),
                 reads=[("hn", hb), "cbf"], writes=[ptr], sig=(k == 7))
        gb = spm[l][:, gcol:gcol + 8].unsqueeze(2).to_broadcast([128, 8, 128])
        P.dve(lambda e: e.tensor_tensor(out=hT[:, :, i * 128:(i + 1) * 128],
                                        in0=pt[:].rearrange("p (k t) -> p k t", k=8), in1=gb, op=ALU.mult),
              reads=[ptr, ("spm", l)], writes=[("hT", i)])

    def transpose_to(src_ap, src_res, nblk, dst_fn, dst_res, eng="act"):
        pt, ptr = tbank()
        for j in range(nblk):
            P.pe(lambda e, j=j: e.transpose(out=pt[:, j * 128:(j + 1) * 128], in_=src_ap[:, j * 128:(j + 1) * 128],
                                            identity=ident),
                 reads=list(src_res) + ["cbf"], writes=[ptr], sig=(j == nblk - 1))
        src_v = pt[:, 0:nblk * 128].rearrange("p (j t) -> p j t", j=nblk)
        if eng == "act":
            P.act(lambda e: e.activation(out=dst_fn, in_=src_v, func=AF.Copy), reads=[ptr], writes=dst_res)
        else:
            P.dve(lambda e: e.tensor_copy(out=dst_fn, in_=src_v), reads=[ptr], writes=dst_res)
        return pt, ptr

    def rotary(l, i, pb, pbr, dst, dst_res, tix):
        rb = tix % 2
        f = cnt["qk"] % 2
        cnt["qk"] += 1
        qf, qfr = Fs[f], ("F", f)
        ta, tar = Fs[2 + f], ("F", 2 + f)
        tb, tbr = Fs[4 + f], ("F", 4 + f)
        P.act(lambda e: e.activation(out=qf[:], in_=pb[:], func=AF.Copy), reads=[pbr], writes=[qfr])
        cosb = rope[rb][:, 0, i, :].unsqueeze(1).unsqueeze(1).to_broadcast([128, 8, 2, 32])
        sinb = rope[rb][:, 1, i, :].unsqueeze(1).to_broadcast([128, 8, 32])
        qv = qf[:].rearrange("p (h a e) -> p h a e", h=8, a=2)
        tAv = ta[:].rearrange("p (h a e) -> p h a e", h=8, a=2)
        tBv = tb[:].rearrange("p (h a e) -> p h a e", h=8, a=2)
        P.dve(lambda e: e.tensor_tensor(out=tAv, in0=qv, in1=cosb, op=ALU.mult),
              reads=[qfr, ("rope", rb)], writes=[tar])
        P.dve(lambda e: e.scalar_tensor_tensor(out=tBv[:, :, 0, :], in0=qv[:, :, 1, :], scalar=-1.0, in1=sinb,
                                               op0=ALU.mult, op1=ALU.mult),
              reads=[qfr, ("rope", rb)], writes=[tbr])
        P.dve(lambda e: e.tensor_tensor(out=tBv[:, :, 1, :], in0=qv[:, :, 0, :], in1=sinb, op=ALU.mult),
              reads=[qfr, ("rope", rb)], writes=[tbr])
        P.dve(lambda e: e.tensor_tensor(out=dst, in0=ta[:], in1=tb[:], op=ALU.add),
              reads=[tar, tbr], writes=dst_res)

    def emit_norm1(l):
        for i in range(4):
            norm_T(l, i, SP_G1T)

    def emit_layer(l, tix, pre_normed=False, mid_hook=None, tail_hook=None):
        sp_ = spm[l]
        if not pre_normed:
            emit_norm1(l)
        hT_res = [("hT", i) for i in range(4)]
        for gi in range(6):
            slot, sres = use_chunk(l, "win%d" % gi)
            ncol = 512 if gi < 5 else 256
            wv = slot[:, 0:8 * ncol].rearrange("p (k c) -> p k c", k=8)
            for i in range(4):
                pb, pbr = bank()
                for k in range(8):
                    P.pe(lambda e, k=k, i=i, pb=pb, wv=wv, ncol=ncol: e.matmul(
                        pb[:, 0:ncol], lhsT=hT[:, k, i * 128:(i + 1) * 128], rhs=wv[:, k, :],
                        start=(k == 0), stop=(k == 7)),
                        reads=[("hT", i), sres], writes=[pbr], sig=(k == 7))
                if gi == 0:
                    f = cnt["qk"] % 2
                    cnt["qk"] += 1
                    vt, vtr = Fs[f], ("F", f)
                    P.act(lambda e: e.activation(out=us[:, i, :], in_=pb[:, 0:256], func=AF.Gelu_apprx_tanh),
                          reads=[pbr], writes=[("us", i)])
                    P.act(lambda e: e.activation(out=vt[:, 0:256], in_=pb[:, 256:512], func=AF.Gelu_apprx_tanh),
                          reads=[pbr], writes=[vtr])
                    ss, ssr = stcol()
                    P.dve(lambda e: e.memset(ss, 0.0), writes=ssr)
                    jk, jkr = junk_()
                    P.act(lambda e: e.activation(out=jk[:, 0:256], in_=vt[:, 0:256], func=AF.Square, accum_out=ss),
                          reads=[vtr] + ssr, writes=[jkr] + ssr)
                    rstd_from_ss(ss, ssr, 1, 1.0 / 256)
                    P.dve(lambda e: e.scalar_tensor_tensor(
                        out=va[:, i, :], in0=vt[:, 0:256], scalar=ss, in1=sp_[:, SP_GV:SP_GV + 256],
                        op0=ALU.mult, op1=ALU.mult),
                        reads=[vtr, ("spm", l)] + ssr, writes=[("va", i)])
                elif gi == 1:
                    rotary(l, i, pb, pbr, qrot[:, i, :], [("big", i)], tix)
                elif gi == 2:
                    rotary(l, i, pb, pbr, krot[:, i, :], [("big", 4 + i)], tix)
                    P.dve(lambda e, i=i: e.tensor_tensor(
                        out=kd[:, i, :].rearrange("p (h d) -> p h d", h=8),
                        in0=krot[:, i, :].rearrange("p (h d) -> p h d", h=8),
                        in1=kd_v.unsqueeze(2).to_broadcast([128, 8, 64]), op=ALU.mult),
                        reads=[("big", 4 + i), "cst"], writes=[("big", 8 + i)])
                elif gi == 3:
                    P.act(lambda e, i=i, pb=pb: e.activation(out=vs[:, i, :], in_=pb[:], func=AF.Copy),
                          reads=[pbr], writes=[("big", 12 + i)])
                elif gi == 4:
                    P.act(lambda e, i=i, pb=pb: e.activation(out=gz[:, i, :], in_=pb[:], func=AF.Silu),
                          reads=[pbr], writes=[("big", 16 + i)])
                    P.pool(lambda e, i=i: e.tensor_tensor(out=gz[:, i, :], in0=gz[:, i, :],
                                                          in1=sp_[:, SP_NG:SP_NG + 512], op=ALU.mult),
                           reads=[("big", 16 + i), ("spm", l)], writes=[("big", 16 + i)])
                else:
                    P.act(lambda e, i=i, pb=pb: e.activation(out=xb[l][:, 1 + i, :], in_=pb[:, 0:256], func=AF.Copy),
                          reads=[pbr], writes=[("xb", l, 1 + i)])
            release()
        if mid_hook is not None:
            mid_hook()
        wa0, wa0r = use_chunk(l, "wouta0")
        wa1, wa1r = use_chunk(l, "wouta1")
        wbb, wbbr = use_chunk(l, "woutb")
        wa0v = wa0[:, :].rearrange("p (c n) -> p c n", c=4)
        wa1v = wa1[:, 0:2048].rearrange("p (c n) -> p c n", c=2)
        wbv = wbb[:, 0:2048].rearrange("p (c n) -> p c n", c=2)
        def stage_a(i):
            ycb = ycbs[i % 2]
            ip = i % 2
            qT, qdT, kT, scT = qTs[ip], qdTs[ip], kTs[ip], scTs[ip]
            P.section = "ret"
            pt_q, ptr_q = transpose_to(qrot[:, i, :], [("big", i)], 4, qT[:], [("qT", ip)], eng="act")
            P.dve(lambda e, pt_q=pt_q: e.tensor_tensor(out=qdT[:], in0=pt_q[:, 0:512].rearrange("p (j t) -> p j t", j=4),
                                                      in1=qd_v, op=ALU.mult),
                  reads=[ptr_q, "cst"], writes=[("qdT", ip)])
            transpose_to(krot[:, i, :], [("big", 4 + i)], 4, kT[:], [("kT", ip)], eng="act")
            P.section = "ret_sc"
            b0, b0r = bank()
            b1, b1r = bank()
            bs_, bsr = (b0, b1), (b0r, b1r)
            for h in range(8):
                j, r = h // 2, h % 2
                P.pe(lambda e, j=j, r=r: e.matmul(bs_[r][:, j * 128:(j + 1) * 128],
                                                  lhsT=kT[64 * r:64 * r + 64, j, :], rhs=qT[64 * r:64 * r + 64, j, :],
                                                  start=True, stop=True, tile_position=(64 * r, 0)),
                     reads=[("kT", ip), ("qT", ip)], writes=[bsr[r]], sig=(h >= 6))
            for r in range(2):
                P.dve(lambda e, r=r: e.tensor_tensor(out=scT[:, r::2, :],
                                                     in0=bs_[r][:].rearrange("p (j t) -> p j t", j=4),
                                                     in1=dm_v[:, r::2, :], op=ALU.mult),
                      reads=[bsr[r], "cst"], writes=[("scT", ip, r)])
            P.section = "ret_kv"
            k0, k0r = bank()
            k1, k1r = bank()
            kb, kbr = (k0, k1), (k0r, k1r)
            for c in range(2):
                for j in range(4):
                    P.pe(lambda e, c=c, j=j: e.matmul(
                        kb[c][:, j * 128:(j + 1) * 128],
                        lhsT=kd[64 * c:64 * c + 64, i, j * 128:(j + 1) * 128],
                        rhs=vs[64 * c:64 * c + 64, i, j * 128:(j + 1) * 128],
                        start=True, stop=True, tile_position=(64 * c, 0)),
                        reads=[("big", 8 + i), ("big", 12 + i)], writes=[kbr[c]], sig=(j == 3))
            for c in range(2):
                n = 2 * i + c
                P.dve(lambda e: e.tensor_tensor(out=Stmp[:], in0=S[l][:],
                                                in1=cd_v.unsqueeze(2).to_broadcast([128, 4, 64]), op=ALU.mult),
                      reads=[("S", l), "cst"], writes=["Stmp"])
                for r in range(2):
                    P.dve(lambda e, c=c, r=r: e.tensor_tensor(
                        out=S[l][64 * r:64 * r + 64, :, :], in0=Stmp[64 * r:64 * r + 64, :, :],
                        in1=kb[c][64 * r:64 * r + 64, :].rearrange("p (j a e) -> p j a e", j=4, a=2)[:, :, r, :],
                        op=ALU.add),
                        reads=["Stmp", kbr[c]], writes=[("S", l)])
                for r in range(2):
                    P.act(lambda e, n=n, r=r: e.activation(out=Sring[64 * r:64 * r + 64, n, r::2, :],
                                                           in_=S[l][64 * r:64 * r + 64, :, :], func=AF.Copy),
                          reads=[("S", l)], writes=[("Sring", n)])
            P.section = "ret_y"
            yb_, ybr = bank()
            for h in range(8):
                j, r = h // 2, h % 2
                P.pe(lambda e, h=h: e.matmul(yb_[:, h * 64:(h + 1) * 64], lhsT=scT[:, h, :],
                                             rhs=vs[:, i, h * 64:(h + 1) * 64], start=True, stop=False),
                     reads=[("scT", ip, r), ("big", 12 + i)], writes=[ybr], sig=False)
                for c in range(2):
                    n = 2 * i + c
                    if n == 0 and L == 1:
                        sst, sstr = Sring[:, 7, h, :], ("Sring", 7)
                    elif n == 0:
                        sst, sstr = Scar[l][:, h, :], ("Scar", l)
                    else:
                        sst, sstr = Sring[:, n - 1, h, :], ("Sring", n - 1)
                    P.pe(lambda e, h=h, j=j, c=c, sst=sst: e.matmul(
                        yb_[64 * c:64 * c + 64, h * 64:(h + 1) * 64], lhsT=qdT[:, j, 64 * c:64 * c + 64], rhs=sst,
                        start=False, stop=True, tile_position=(0, 64 * c)),
                        reads=[("qdT", ip), sstr], writes=[ybr], sig=(h == 7 and c == 1))
            P.section = "ret_gn"
            yv = yb_[:].rearrange("p (h e) -> p h e", h=8)
            s1, s1r = stcol(8)
            s2, s2r = stcol(8)
            mq, mqr = stcol(8)
            P.dve(lambda e: e.tensor_reduce(out=s1, in_=yv, axis=AX.X, op=ALU.add), reads=[ybr], writes=s1r)
            ysq, yt1, yt2, yat = Fs[0], Fs[1], Fs[2], Fs[3]
            P.act(lambda e: e.activation(out=ysq[:], in_=yb_[:], func=AF.Square), reads=[ybr], writes=[("F", 0)])
            P.dve(lambda e: e.tensor_reduce(out=s2, in_=ysq[:].rearrange("p (h e) -> p h e", h=8), axis=AX.X,
                                            op=ALU.add), reads=[("F", 0)], writes=s2r)
            P.dve(lambda e: e.tensor_scalar(out=s1, in0=s1, scalar1=1.0 / 64, scalar2=None, op0=ALU.mult),
                  reads=s1r, writes=s1r)
            P.dve(lambda e: e.tensor_tensor(out=mq, in0=s1, in1=s1, op=ALU.mult), reads=s1r, writes=mqr)
            P.dve(lambda e: e.scalar_tensor_tensor(out=s2, in0=s2, scalar=1.0 / 64, in1=mq, op0=ALU.mult,
                                                   op1=ALU.subtract), reads=s2r + mqr, writes=s2r)
            P.dve(lambda e: e.tensor_scalar(out=s2, in0=s2, scalar1=EPS, scalar2=None, op0=ALU.add),
                  reads=s2r, writes=s2r)
            P.dve(lambda e: e.reciprocal(out=s2, in_=s2), reads=s2r, writes=s2r)
            P.act(lambda e: e.activation(out=s2, in_=s2, func=AF.Sqrt), reads=s2r, writes=s2r)
            P.dve(lambda e: e.tensor_tensor(out=yt1[:].rearrange("p (h e) -> p h e", h=8), in0=yv,
                                            in1=s1.unsqueeze(2).to_broadcast([128, 8, 64]), op=ALU.subtract),
                  reads=[ybr] + s1r, writes=[("F", 1)])
            P.dve(lambda e: e.tensor_tensor(out=yt2[:].rearrange("p (h e) -> p h e", h=8),
                                            in0=yt1[:].rearrange("p (h e) -> p h e", h=8),
                                            in1=s2.unsqueeze(2).to_broadcast([128, 8, 64]), op=ALU.mult),
                  reads=[("F", 1)] + s2r, writes=[("F", 2)])
            P.dve(lambda e: e.tensor_tensor(out=ycb[:], in0=yt2[:], in1=gz[:, i, :], op=ALU.mult),
                  reads=[("F", 2), ("big", 16 + i)], writes=[("ycb", i % 2)])

        def stage_b(i):
            ycb = ycbs[i % 2]
            P.section = "ret_gn"
            transpose_to(ycb[:], [("ycb", i % 2)], 4, ycT[:, :, i * 128:(i + 1) * 128], [("ycT", i)], eng="act")
            P.section = "A"
            yat = Fs[3]
            ab, abr = bank()
            for h in range(4):
                P.pe(lambda e, h=h: e.matmul(ab[:, h * 64:(h + 1) * 64], lhsT=wmT[l][:, h, :],
                                             rhs=va[:, i, h * 64:(h + 1) * 64], start=True, stop=True),
                     reads=[("wmT", l), ("va", i)], writes=[abr], sig=(h == 3))
            P.dve(lambda e: e.tensor_tensor(out=yat[:, 0:256].rearrange("p (h d) -> p h d", h=4),
                                            in0=ab[:, 0:256].rearrange("p (h d) -> p h d", h=4),
                                            in1=sp_[:, SP_BST:SP_BST + 4].unsqueeze(2).to_broadcast([128, 4, 64]),
                                            op=ALU.add),
                  reads=[abr, ("spm", l)], writes=[("F", 3)])
            P.dve(lambda e: e.tensor_tensor(out=yab[:], in0=yat[:, 0:256], in1=us[:, i, :], op=ALU.mult),
                  reads=[("F", 3), ("us", i)], writes=["yab"])
            transpose_to(yab[:], ["yab"], 2, yaT[:, :, i * 128:(i + 1) * 128], [("yaT", i)], eng="act")
            P.section = "B"
            pbk, pbkr = bank()
            if i == 0 and tix == 0:
                pdiag = ps0_v
            elif i == 0 and tix == lag and lag > 0:
                pdiag = psl_v
            else:
                pdiag = pm_v[:, 0]
            for g in range(4):
                P.pe(lambda e, g=g: e.matmul(pbk[0:64, g * 128:(g + 1) * 128],
                                             lhsT=xb[l][:, 1 + i, g * 64:(g + 1) * 64],
                                             rhs=pdiag[:, g, :], start=True, stop=False),
                     reads=[("xb", l, 1 + i), "cbf"], writes=[pbkr], sig=False)
                P.pe(lambda e, g=g: e.matmul(pbk[0:64, g * 128:(g + 1) * 128],
                                             lhsT=xb[l][:, i, g * 64:(g + 1) * 64],
                                             rhs=pm_v[:, 1, g, :], start=False, stop=True),
                     reads=[("xb", l, i), "cbf"], writes=[pbkr], sig=(g == 3))
            P.act(lambda e: e.activation(out=plb[:], in_=pbk[0:64, :], func=AF.Copy), reads=[pbkr], writes=["plb"])
            pb2, pb2r = bank()
            for g in range(4):
                hf, pr = g % 2, g // 2
                P.pe(lambda e, g=g, hf=hf, pr=pr: e.matmul(
                    pb2[64 * hf:64 * hf + 64, pr * 128:(pr + 1) * 128], lhsT=bw[l][:, g, :],
                    rhs=plb[:, g * 128:(g + 1) * 128], start=True, stop=True, tile_position=(0, 64 * hf)),
                    reads=[("bw", l), "plb"], writes=[pb2r], sig=(g == 3))
            P.dve(lambda e: e.tensor_tensor(out=ybT[:, :, i * 128:(i + 1) * 128],
                                            in0=pb2[:, 0:256].rearrange("p (g t) -> p g t", g=2),
                                            in1=sp_[:, SP_BSC:SP_BSC + 2].unsqueeze(2).to_broadcast([128, 2, 128]),
                                            op=ALU.mult),
                  reads=[pb2r, ("spm", l)], writes=[("ybT", i)])
            P.section = "wo"
            for n in range(2):
                ob_, obr = bank()
                mm = []
                for c in range(2):
                    mm.append((yaT[:, c, i * 128:(i + 1) * 128], wa0v[:, c, n * 512:(n + 1) * 512], ("yaT", i), wa0r))
                for c in range(2):
                    mm.append((ycT[:, c, i * 128:(i + 1) * 128], wa0v[:, 2 + c, n * 512:(n + 1) * 512], ("ycT", i), wa0r))
                for c in range(2):
                    mm.append((ycT[:, 2 + c, i * 128:(i + 1) * 128], wa1v[:, c, n * 512:(n + 1) * 512], ("ycT", i), wa1r))
                for g in range(2):
                    mm.append((ybT[:, g, i * 128:(i + 1) * 128], wbv[:, g, n * 512:(n + 1) * 512], ("ybT", i), wbbr))
                for q, (lh, rh, r1, r2) in enumerate(mm):
                    P.pe(lambda e, lh=lh, rh=rh, q=q: e.matmul(ob_[:], lhsT=lh, rhs=rh, start=(q == 0),
                                                               stop=(q == len(mm) - 1)),
                         reads=[r1, r2], writes=[obr], sig=(q == len(mm) - 1))
                P.dve(lambda e, n=n: e.tensor_tensor(out=xs_()[:, i, n * 512:(n + 1) * 512],
                                                     in0=xs_()[:, i, n * 512:(n + 1) * 512], in1=ob_[:], op=ALU.add),
                      reads=[xr(i), obr], writes=[xr(i)])

        for (st_, i_) in (("a", 0), ("a", 1), ("b", 0), ("a", 2), ("b", 1), ("a", 3), ("b", 2), ("b", 3)):
            (stage_a if st_ == "a" else stage_b)(i_)
        P.section = None
        release(3)
        P.dve(lambda e: e.tensor_copy(out=xb[l][:, 0, :], in_=xb[l][:, 4, :]),
              reads=[("xb", l, 4)], writes=[("xb", l, 0)])
        if L > 1:
            P.pool(lambda e: e.tensor_copy(out=Scar[l][:], in_=Sring[:, 7, :, :]),
                   reads=[("Sring", 7)], writes=[("Scar", l)])
        P.section = "ffn"
        for i in range(4):
            norm_T(l, i, SP_G2T)
        cwv = sp_[:, SP_CW:SP_CW + 176].rearrange("p (j f) -> p j f", f=4)
        P.dve(lambda e: e.tensor_tensor(out=hct[:], in0=halo[l][:, :, 1], in1=cwv[:, :, 1], op=ALU.mult),
              reads=[("halo", l), ("spm", l)], writes=["hct"])
        P.dve(lambda e: e.tensor_tensor(out=hc[:, :, 0], in0=halo[l][:, :, 0], in1=cwv[:, :, 0], op=ALU.mult),
              reads=[("halo", l), ("spm", l)], writes=[("hc", 0)])
        P.dve(lambda e: e.tensor_tensor(out=hc[:, :, 0], in0=hc[:, :, 0], in1=hct[:], op=ALU.add),
              reads=[("hc", 0), "hct"], writes=[("hc", 0)])
        P.dve(lambda e: e.tensor_tensor(out=hc[:, :, 1], in0=halo[l][:, :, 1], in1=cwv[:, :, 0], op=ALU.mult),
              reads=[("halo", l), ("spm", l)], writes=[("hc", 1)])
        for c in range(11):
            slot, sres = use_chunk(l, "wup%d" % c)
            wv = slot[:, :].rearrange("p (a k f) -> p a k f", a=2, k=8)
            for a in range(2):
                jj = 2 * c + a
                f = cnt["cg"] % 3
                cnt["cg"] += 1
                res = []
                pbs = []
                for gv in range(2):
                    pb, pbr = bank()
                    halves = ((0, 256), (256, 512)) if jj < 2 else ((0, 512),)
                    for (t0_, t1_) in halves:
                        hres = [("hT", q) for q in range(t0_ // 128, t1_ // 128)]
                        for k in range(8):
                            P.pe(lambda e, k=k, a=a, gv=gv, pb=pb, wv=wv, t0_=t0_, t1_=t1_: e.matmul(
                                pb[:, t0_:t1_], lhsT=wv[:, a, k, gv * 128:(gv + 1) * 128], rhs=hT[:, k, t0_:t1_],
                                start=(k == 0), stop=(k == 7)),
                                reads=hres + [sres], writes=[pbr], sig=(k == 7))
                    pbs.append((pb, pbr, jj + 22 * gv, Fs[2 * f + gv], ("F", 2 * f + gv)))
                    res.append(("F", 2 * f + gv))
                for (pb, pbr, jg, cb, cbr) in pbs:
                    P.act(lambda e, pb=pb, cb=cb, jg=jg: e.activation(out=cb[:], in_=pb[:], func=AF.Identity,
                                                                      scale=cwv[:, jg, 2:3], bias=cwv[:, jg, 3:4]),
                          reads=[pbr, ("spm", l)], writes=[cbr])
                for (pb, pbr, jg, cb, cbr) in pbs:
                    P.dve(lambda e, pb=pb, cb=cb, jg=jg: e.scalar_tensor_tensor(
                        out=cb[:, 1:TILE], in0=pb[:, 0:TILE - 1], scalar=cwv[:, jg, 1:2], in1=cb[:, 1:TILE],
                        op0=ALU.mult, op1=ALU.add), reads=[pbr, cbr, ("spm", l)], writes=[cbr])
                for (pb, pbr, jg, cb, cbr) in pbs:
                    P.dve(lambda e, pb=pb, cb=cb, jg=jg: e.scalar_tensor_tensor(
                        out=cb[:, 2:TILE], in0=pb[:, 0:TILE - 2], scalar=cwv[:, jg, 0:1], in1=cb[:, 2:TILE],
                        op0=ALU.mult, op1=ALU.add), reads=[pbr, cbr, ("spm", l)], writes=[cbr])
                for (pb, pbr, jg, cb, cbr) in pbs:
                    P.act(lambda e, pb=pb, jg=jg: e.activation(out=halo[l][:, jg, :], in_=pb[:, TILE - 2:TILE],
                                                               func=AF.Copy),
                          reads=[pbr], writes=[("halo", l)])
                for (pb, pbr, jg, cb, cbr) in pbs:
                    P.pool(lambda e, cb=cb, jg=jg: e.tensor_tensor(out=cb[:, 0:2], in0=cb[:, 0:2], in1=hc[:, jg, :],
                                                                   op=ALU.add),
                           reads=[cbr, ("hc", 0), ("hc", 1)], writes=[cbr])
                P.act(lambda e, f=f: e.activation(out=Fs[2 * f][:], in_=Fs[2 * f][:], func=AF.Silu),
                      reads=[res[0]], writes=[res[0]])
                P.pool(lambda e, f=f, jj=jj: e.tensor_tensor(out=actT[:, jj, :], in0=Fs[2 * f][:], in1=Fs[2 * f + 1][:],
                                                             op=ALU.mult),
                       reads=res, writes=[("big", jj)])
            release()
        for n in range(2):
            accs = [bank() for _ in range(4)]
            for c in range(3):
                slot, sres = use_chunk(l, "wdn%d_%d" % (n, c))
                nj = 8 if c < 2 else 6
                wv = slot[:, 0:nj * 512].rearrange("p (j f) -> p j f", j=nj)
                for i in range(4):
                    for q in range(nj):
                        jj = 8 * c + q
                        P.pe(lambda e, i=i, q=q, jj=jj, wv=wv: e.matmul(
                            accs[i][0][:], lhsT=actT[:, jj, i * 128:(i + 1) * 128], rhs=wv[:, q, :],
                            start=(jj == 0), stop=(jj == NJ - 1)),
                            reads=[("big", jj), sres], writes=[accs[i][1]], sig=(q == nj - 1))
                release()
            if n == 1 and tail_hook is not None:
                P.section = None
                tail_hook()
                P.section = "ffn"
            for i in range(4):
                P.dve(lambda e, i=i, n=n: e.tensor_tensor(out=xs_()[:, i, n * 512:(n + 1) * 512],
                                                          in0=xs_()[:, i, n * 512:(n + 1) * 512], in1=accs[i][0][:],
                                                          op=ALU.add),
                      reads=[xr(i), accs[i][1]], writes=[xr(i)])
        P.section = None

    P.section = None
    out_chans = []
    def load_input(tix):
        par = tix % 2
        xb_ = xsb[par]
        P.dma("pool", xb_[:], x_d[tix * TILE:(tix + 1) * TILE, :].rearrange("(i p) d -> p i d", p=128),
              writes=[("x", par, i) for i in range(4)], chan=("xload", par))
        rb = tix % 2
        P.dma("pool", rope[rb][:], ROPE_d[tix].rearrange("p (a i e) -> p a i e", a=2, i=4),
              writes=[("rope", rb)], chan=("rope", rb))

    def load_sel(tix):
        par = tix % 2
        xb_ = xsb[par]
        if pipe and tix >= lag:
            gp = (tix - lag) % 2
            for i in range(4):
                gb = cnt["x"] % 2
                cnt["x"] += 1
                P.dma("pool", xg[gb][:], gath_d[gp][i * 128:(i + 1) * 128, :], reads=[("gath", gp)],
                      writes=[("xg", gb)], chan=("xg", gb))
                P.dve(lambda e, i=i, gb=gb: e.scalar_tensor_tensor(out=xb_[:, i, :], in0=xg[gb][:], scalar=role[:, 1:2],
                                                                   in1=xb_[:, i, :], op0=ALU.mult, op1=ALU.add),
                      reads=[("x", par, i), ("xg", gb), "role"], writes=[("x", par, i)])

    load_input(0)
    load_sel(0)
    cx["p"] = 0
    emit_norm1(0)
    for tix in range(n_tiles):
        cx["p"] = tix % 2
        has_next = tix + 1 < n_tiles

        def mid_hook(tix=tix):
            if tix + 1 < n_tiles:
                load_input(tix + 1)

        def tail_hook(tix=tix):
            if tix + 1 < n_tiles:
                load_sel(tix + 1)
                cx["p"] = (tix + 1) % 2
                emit_norm1(0)
                cx["p"] = tix % 2

        for l in range(L):
            emit_layer(l, tix, pre_normed=(l == 0), mid_hook=(mid_hook if l == 0 else None),
                       tail_hook=(tail_hook if l == L - 1 else None))
        sp_ = tix % 2
        for i in range(4):
            o = cnt["ob"] % 2
            cnt["ob"] += 1
            if final_norm:
                ss, ssr = stcol()
                P.dve(lambda e, ss=ss: e.memset(ss, 0.0), writes=ssr)
                jk, jkr = junk_()
                P.act(lambda e, i=i, ss=ss: e.activation(out=jk[:], in_=xs_()[:, i, :], func=AF.Square, accum_out=ss),
                      reads=[xr(i)] + ssr, writes=[jkr] + ssr)
                rstd_from_ss(ss, ssr, 1, 1.0 / D)
                P.dve(lambda e, ss=ss: e.tensor_scalar(out=ss, in0=ss, scalar1=role[:, 2:3], scalar2=role[:, 3:4],
                                                       op0=ALU.mult, op1=ALU.add),
                      reads=ssr + ["role"], writes=ssr)
                P.dve(lambda e, i=i, ss=ss, o=o: e.scalar_tensor_tensor(out=ob[o][:], in0=xs_()[:, i, :], scalar=ss,
                                                                        in1=gf[:], op0=ALU.mult, op1=ALU.mult),
                      reads=[xr(i), "gf", "role"] + ssr, writes=[("ob", o)])
            else:
                P.act(lambda e, i=i, o=o: e.activation(out=ob[o][:], in_=xs_()[:, i, :], func=AF.Copy),
                      reads=[xr(i)], writes=[("ob", o)])
            r0 = tix * TILE + i * 128
            if pipe and tix < n_tiles - lag:
                P.dma("pool", stage_d[sp_][i * 128:(i + 1) * 128, :], ob[o][:], reads=[("ob", o)],
                      writes=[("stage", sp_, i)], chan=("sst", o))
            P.dma("pool", out_d[r0:r0 + 128, :], ob[o][:], reads=[("ob", o)], chan=("ost", o))
            if ("ost", o) not in out_chans:
                out_chans.append(("ost", o))
        if pipe and tix < n_tiles - lag:
            P.async_op("pool", lambda e: e.collective_compute("AllGather", ALU.bypass, replica_groups=GROUPS,
                                                              ins=[stage_d[sp_]], outs=[gath_d[sp_]]),
                       reads=[("stage", sp_, i) for i in range(4)], writes=[("gath", sp_)], chan=("ag", sp_), inc=1)
    assert wst["used"] == len(seq)
    print("[build] sbuf bytes remaining per partition:", nc.sbuf_bytes_remaining, "ops:", len(P.ops))
    P.finalize(final_wait_chans=out_chans)
    return nc, len(P.ops)


def _layer_inputs(slot, li, p):
    W, WB = _prep_layer(p["w_in"][li], p["w_out"][li], p["w_up"][li], p["w_down"][li])
    SPm = _prep_small(p["norm1_g"][li], p["norm2_g"][li], p["a_vnorm_g"][li], p["c_norm_g"][li], p["a_bs"][li],
                      p["conv_w"][li], p["conv_b"][li], p["b_scale"][li])
    WS = np.ascontiguousarray(p["a_ws"][li].transpose(2, 0, 1).reshape(128, 512), dtype=np.float32)
    BW = np.ascontiguousarray(p["b_w"][li].transpose(1, 0, 2).reshape(64, 256), dtype=np.float32)
    return {"W%d" % slot: W, "WB%d" % slot: WB, "SP%d" % slot: SPm, "WS%d" % slot: WS, "BW%d" % slot: BW}


_PROG_CACHE = {}
LAG = 2


def _get_prog(key, *args, **kw):
    if key not in _PROG_CACHE:
        _PROG_CACHE[key] = build_program(*args, **kw)[0]
    return _PROG_CACHE[key]


def run_model(x, p, mode="pipe"):
    B, S, _ = x.shape
    depth = p["w_in"].shape[0]
    n_tiles = S // TILE
    cst, cbf, rope = _constants(n_tiles)
    rope = rope.reshape(n_tiles, 128, 256)
    GF = np.ascontiguousarray(np.broadcast_to(p["final_g"][None, :], (128, D)), dtype=np.float32)
    role_last = np.tile(np.array([1, 0, 1, 0], np.float32)[None, :], (128, 1))
    if mode == "pipe":
        assert depth == 2 and 2 * B == 8
        n_steps = n_tiles + LAG
        nc = _get_prog(("pipe", n_tiles), n_steps, [0], final_norm=True, lag=LAG, pipe=True)
        cbf1 = cbf.copy()
        cbf1[:, CBF_PS0:CBF_PS0 + 512] = cbf[:, CBF_PSL:CBF_PSL + 512]
        cbf1[:, CBF_PSL:CBF_PSL + 512] = cbf[:, CBF_PS0:CBF_PS0 + 512]
        zt = np.zeros((LAG, 128, 256), np.float32)
        per_role = [
            dict(GF=np.ones((128, D), np.float32), CST=cst, CBF=cbf, ROPE=np.concatenate([rope, zt], axis=0),
                 ROLE=np.tile(np.array([1, 0, 0, 1], np.float32)[None, :], (128, 1))),
            dict(GF=GF, CST=cst, CBF=cbf1, ROPE=np.concatenate([zt, rope], axis=0),
                 ROLE=np.tile(np.array([0, 1, 1, 0], np.float32)[None, :], (128, 1))),
        ]
        for r in range(2):
            per_role[r].update(_layer_inputs(0, r, p))
        zx = np.zeros((n_steps * TILE, D), np.float32)
        in_maps = []
        for c in range(2 * B):
            b, r = c // 2, c % 2
            m = dict(per_role[r])
            if r == 0:
                m["x"] = np.concatenate([x[b], np.zeros((LAG * TILE, D), np.float32)], axis=0)
            else:
                m["x"] = zx
            in_maps.append(m)
        res = run_bass_kernel_spmd(nc, in_maps, core_ids=list(range(2 * B)))
        return np.stack([np.asarray(res.results[2 * b + 1]["out"], dtype=np.float32)[LAG * TILE:] for b in range(B)],
                        axis=0)
    common = {"GF": GF, "CST": cst, "CBF": cbf, "ROPE": rope, "ROLE": role_last}
    launches = [list(range(depth))] if mode == "fused" else [[l] for l in range(depth)]
    cur = [np.ascontiguousarray(x[b], dtype=np.float32) for b in range(B)]
    for li, layer_ids in enumerate(launches):
        last = (li == len(launches) - 1)
        nc = _get_prog((n_tiles, len(layer_ids), last), n_tiles, list(range(len(layer_ids))), final_norm=last)
        base = dict(common)
        for slot, l in enumerate(layer_ids):
            base.update(_layer_inputs(slot, l, p))
        in_maps = []
        for b in range(B):
            m = dict(base)
            m["x"] = cur[b]
            in_maps.append(m)
        res = run_bass_kernel_spmd(nc, in_maps, core_ids=list(range(B)))
        cur = [np.asarray(res.results[b]["out"], dtype=np.float32) for b in range(B)]
    return np.stack(cur, axis=0)


MODE = "pipe"


def kernel(x, norm1_g, w_in, a_vnorm_g, a_ws, a_bs, b_w, b_scale, c_norm_g,
           w_out, norm2_g, w_up, conv_w, conv_b, w_down, final_g):
    p = dict(norm1_g=norm1_g, w_in=w_in, a_vnorm_g=a_vnorm_g, a_ws=a_ws, a_bs=a_bs, b_w=b_w, b_scale=b_scale,
             c_norm_g=c_norm_g, w_out=w_out, norm2_g=norm2_g, w_up=w_up, conv_w=conv_w, conv_b=conv_b,
             w_down=w_down, final_g=final_g)
    p = {k: np.asarray(v, dtype=np.float32) for k, v in p.items()}
    return run_model(np.asarray(x, dtype=np.float32), p, mode=MODE)
```

```python
import contextlib
import numpy as np
import concourse.bass as bass
import concourse.mybir as mybir
from concourse.bass_utils import run_bass_kernel_spmd

F32 = mybir.dt.float32
BF16 = mybir.dt.bfloat16
AF = mybir.ActivationFunctionType
ALU = mybir.AluOpType
AX = mybir.AxisListType

D = 1024
DFF = 2816
NJ = 22
EPS = 1e-6
TILE = 512
NRING = 5
import os as _os
PENW = float(_os.environ.get("KPENW", "1.0"))
ELIG = float(_os.environ.get("KELIG", "0.2"))
SLOT = 4096


class _Op:
    __slots__ = ("eng", "fn", "deps", "dma", "chan", "sig", "idx", "tok", "inc")

    def __init__(self):
        self.tok = None


class _Rec:
    def __getattr__(self, name):
        def f(*a, **k):
            self.__dict__["call"] = (name, a, k)
            return self
        return f


class Prog:
    COMPUTE = ("pe", "act", "dve", "pool")

    def __init__(self, nc):
        self.nc = nc
        self.ops = []
        self.last_write = {}
        self.readers = {}
        self.section = None
        import os
        self.skip = set(filter(None, os.environ.get("KSKIP", "").split(",")))

    def add(self, eng, fn, reads=(), writes=(), dma=False, chan=None, sig=True, inc=16):
        if self.section in self.skip:
            return None
        op = _Op()
        rec = _Rec()
        fn(rec)
        op.eng, op.fn, op.dma, op.chan, op.sig = eng, rec.call, dma, chan, sig
        op.inc = inc
        op.idx = len(self.ops)
        pr = [r for r in reads if isinstance(r, tuple) and r[0] in ("psF", "psT")]
        if pr:
            reads = [r for r in reads if r not in pr]
            writes = list(writes) + pr
        deps = {}
        for r in reads:
            w = self.last_write.get(r)
            if w is not None:
                deps[w.idx] = (w, True)
        for r in writes:
            w = self.last_write.get(r)
            if w is not None and w.idx not in deps:
                deps[w.idx] = (w, False)
            for rd in self.readers.get(r, ()):
                if rd.idx not in deps and rd is not op:
                    deps[rd.idx] = (rd, False)
        op.deps = list(deps.values())
        for r in reads:
            self.readers.setdefault(r, []).append(op)
        for r in writes:
            self.last_write[r] = op
            self.readers[r] = []
        self.ops.append(op)
        return op

    def pe(self, fn, reads=(), writes=(), sig=True):
        return self.add("pe", fn, reads, writes, sig=sig)

    def act(self, fn, reads=(), writes=()):
        return self.add("act", fn, reads, writes)

    def dve(self, fn, reads=(), writes=()):
        return self.add("dve", fn, reads, writes)

    def pool(self, fn, reads=(), writes=()):
        return self.add("pool", fn, reads, writes)

    def dma(self, queue, out, in_, reads=(), writes=(), chan=None, **kw):
        assert chan is not None
        return self.add(queue, lambda e: e.dma_start(out=out, in_=in_, **kw),
                        reads, list(writes) + [("chan", chan)], dma=True, chan=chan)

    def async_op(self, queue, fn, reads=(), writes=(), chan=None, inc=1):
        return self.add(queue, fn, reads, list(writes) + [("chan", chan)], dma=True, chan=chan, inc=inc)

    @staticmethod
    def _free(ap):
        n = 1
        for d in ap.shape[1:]:
            n *= d
        return n

    def _dur(self, op):
        name, a, k = op.fn
        if op.dma:
            if name != "dma_start":
                return 0.7, 90.0
            o = k["out"]
            nbytes = o.shape[0] * self._free(o) * (2 if o.dtype == BF16 else 4)
            return (0.65 if op.eng == "pool" else 0.15), 2.0 + nbytes / 160e3
        if op.eng == "pe":
            if name == "transpose":
                return 0.1, 0.1
            n = self._free(k["rhs"])
            d = max(n / 2400.0 + 0.005, 0.11)
            return d, d
        out = k.get("out", k.get("ap", None))
        n = self._free(out) if out is not None else 64
        if op.eng == "act":
            d = 0.22 + n * 0.9e-3
        elif op.eng == "dve":
            d = 0.08 + n * 1.25e-3
        else:
            d = 0.3 + n * 2.2e-3
        return d, d

    def schedule(self, window=32):
        ops = self.ops
        units = []
        unit_of = {}
        cur = None
        for op in ops:
            if op.eng == "pe" and not op.dma:
                if cur is None:
                    cur = [op.eng, [], set(), 0.0, 0.0, op.idx]
                    units.append(cur)
                cur[1].append(op)
                unit_of[op.idx] = len(units) - 1
                b, _ = self._dur(op)
                cur[3] += b
                cur[4] = cur[3]
                if op.sig:
                    cur = None
            else:
                b, lat = self._dur(op)
                units.append([op.eng, [op], set(), b, lat, op.idx])
                unit_of[op.idx] = len(units) - 1
        assert cur is None
        for ui, u in enumerate(units):
            for op in u[1]:
                for (d, raw) in op.deps:
                    du = unit_of[d.idx]
                    if du != ui:
                        u[2].add(du)
        blev = [0.0] * len(units)
        for ui in range(len(units) - 1, -1, -1):
            u = units[ui]
            mine = blev[ui] + u[4]
            for d in u[2]:
                if mine > blev[d]:
                    blev[d] = mine
        pend = {}
        for ui, u in enumerate(units):
            pend.setdefault(u[0], []).append(ui)
        ptr = {e: 0 for e in pend}
        TSET = {str(AF.Sqrt): "sqrt", str(AF.Silu): "silu", str(AF.Gelu_apprx_tanh): "gelu"}
        act_set = [None]

        def tset(ui):
            u = units[ui]
            if u[0] != "act":
                return None
            return TSET.get(str(u[1][0].fn[2].get("func", "")))
        done = {}
        free_at = {e: 0.0 for e in pend}
        flex = ("pe", "act", "dve")
        order = {e: [] for e in pend}
        taken = [False] * len(units)
        nleft = len(units)
        SEM = 0.25

        def ready_time(ui):
            u = units[ui]
            t = 0.0
            for d in u[2]:
                f = done.get(d)
                if f is None:
                    return None
                if units[d][0] != u[0]:
                    f += SEM
                if f > t:
                    t = f
            return t

        while nleft:
            best = None
            for e, lst in pend.items():
                p = ptr[e]
                while p < len(lst) and taken[lst[p]]:
                    p += 1
                ptr[e] = p
                if p >= len(lst):
                    continue
                cand = None
                lim = window if e in flex else 1
                seen = 0
                q = p
                cl = []
                while q < len(lst) and seen < lim:
                    ui = lst[q]
                    q += 1
                    if taken[ui]:
                        continue
                    seen += 1
                    r = ready_time(ui)
                    if r is None:
                        continue
                    st = max(r, free_at[e])
                    pen = 0.0
                    if e == "act":
                        ts = tset(ui)
                        if ts is not None and ts != act_set[0]:
                            pen = 1.3
                    cl.append((st, ui, pen))
                if cl:
                    mst = min(c[0] for c in cl)
                    c = max((c for c in cl if c[0] <= mst + ELIG), key=lambda c: (blev[c[1]] - PENW * c[2], -c[1]))
                    cand = (c[0] + c[2], c[1])
                if cand is not None and (best is None or cand[0] < best[0]):
                    best = (cand[0], cand[1], e)
            assert best is not None, "scheduler deadlock"
            st, ui, e = best
            u = units[ui]
            taken[ui] = True
            nleft -= 1
            if e == "act":
                ts = tset(ui)
                if ts is not None:
                    act_set[0] = ts
            free_at[e] = st + u[3]
            done[ui] = st + u[4]
            order[e].append(ui)
            if getattr(self, "timeline", None) is not None:
                self.timeline.append((e, st, u[3], u[1][0].fn[0], u[5], ui))
        self.sched_makespan = max(done.values())
        self._units, self._done, self._unit_of = units, done, unit_of
        eng_ops = {}
        pos = {}
        n = 0
        for e, lst in order.items():
            eng_ops[e] = []
            for ui in lst:
                for op in units[ui][1]:
                    eng_ops[e].append(op)
        return eng_ops

    def finalize(self, final_wait_chans=(), schedule=True):
        nc = self.nc
        ops = self.ops
        if schedule:
            eng_ops = self.schedule()
        else:
            eng_ops = {}
            for op in ops:
                eng_ops.setdefault(op.eng, []).append(op)
        spos = {}
        for e, lst in eng_ops.items():
            for i, op in enumerate(lst):
                spos[op.idx] = i
        chans = []
        seen_c = set()
        for op in ops:
            if op.dma and op.chan not in seen_c:
                seen_c.add(op.chan)
                chans.append(op.chan)
        sem_keys = [("e", e) for e in self.COMPUTE] + [("c", c) for c in chans]
        stack = contextlib.ExitStack()
        sems = {}
        for i, key in enumerate(sem_keys):
            sems[key] = stack.enter_context(nc.semaphore("s%d" % i))
        for e in self.COMPUTE:
            cnt = 0
            pend = []
            for o in eng_ops.get(e, []):
                if o.dma:
                    continue
                pend.append(o)
                if o.sig:
                    cnt += 1
                    for p in pend:
                        p.tok = (("e", e), cnt, o.idx)
                    pend = []
            assert not pend, "trailing non-sig ops on %s" % e
        ccount = {}
        for op in ops:
            if op.dma:
                ccount[op.chan] = ccount.get(op.chan, 0) + op.inc
                op.tok = (("c", op.chan), ccount[op.chan], op.idx)

        def emit_stream(e, handle):
            seen = {}
            for op in eng_ops.get(e, []):
                for (d, raw) in op.deps:
                    if (not d.dma) and (not op.dma) and d.eng == e and e == "pe":
                        continue
                    key, val, sidx = d.tok
                    if seen.get(key, 0) >= val:
                        continue
                    seen[key] = val
                    handle.wait_ge(sems[key], val)
                name, a, k = op.fn
                ins = getattr(handle, name)(*a, **k)
                if op.dma:
                    ins.then_inc(sems[op.tok[0]], op.inc)
                elif op.sig:
                    ins.then_inc(sems[("e", e)], 1)
            if e == "sp":
                for c in final_wait_chans:
                    handle.wait_ge(sems[("c", c)], ccount[c])

        with nc.Block() as block:
            @block.sync
            def _(h):
                emit_stream("sp", h)

            @block.tensor
            def _(h):
                emit_stream("pe", h)

            @block.scalar
            def _(h):
                emit_stream("act", h)

            @block.vector
            def _(h):
                emit_stream("dve", h)

            @block.gpsimd
            def _(h):
                emit_stream("pool", h)
        stack.close()


def _chunk_table():
    t = []
    off = 0
    for gi in range(6):
        w = 4096 if gi < 5 else 2048
        t.append(("win%d" % gi, "W", off, w, 128))
        off += w
    t.append(("wouta0", "W", off, 4096, 128)); off += 4096
    t.append(("wouta1", "W", off, 2048, 128)); off += 2048
    t.append(("woutb", "WB", 0, 2048, 128))
    for c in range(11):
        t.append(("wup%d" % c, "W", off, 4096, 128)); off += 4096
    for n in range(2):
        for c in range(3):
            w = 4096 if c < 2 else 6 * 512
            t.append(("wdn%d_%d" % (n, c), "W", off, w, 128)); off += w
    return t, off


CHUNKS, WCOLS = _chunk_table()


def _prep_layer(w_in, w_out, w_up, w_down):
    parts = []
    wi = w_in.reshape(8, 128, 2816).transpose(1, 0, 2)
    groups = [(0, 512), (768, 1280), (1280, 1792), (1792, 2304), (2304, 2816), (512, 768)]
    for (a, b) in groups:
        parts.append(wi[:, :, a:b].reshape(128, -1))
    wa = np.concatenate([w_out[0:256], w_out[512:1024]], axis=0).reshape(6, 128, 1024).transpose(1, 0, 2)
    parts.append(wa[:, 0:4].reshape(128, -1))
    parts.append(wa[:, 4:6].reshape(128, -1))
    wb = w_out[256:512].reshape(2, 2, 64, 1024).transpose(1, 2, 0, 3).reshape(128, 2048)
    wu = w_up.reshape(8, 128, 2, NJ, 128).transpose(1, 3, 0, 2, 4)
    for c in range(11):
        parts.append(wu[:, 2 * c:2 * c + 2].reshape(128, -1))
    wd = w_down.reshape(NJ, 128, 2, 512).transpose(1, 2, 0, 3)
    for n in range(2):
        for (a, b) in ((0, 8), (8, 16), (16, 22)):
            parts.append(wd[:, n, a:b].reshape(128, -1))
    W = np.ascontiguousarray(np.concatenate(parts, axis=1), dtype=np.float32)
    assert W.shape == (128, WCOLS)
    return W, np.ascontiguousarray(wb, dtype=np.float32)


SP_G1T, SP_G2T, SP_GV, SP_NG, SP_BST, SP_CW, SP_BSC = 0, 8, 16, 272, 784, 788, 964
SPW = 968


def _prep_small(norm1_g, norm2_g, a_vnorm_g, c_norm_g, a_bs, conv_w, conv_b, b_scale):
    sp = np.zeros((128, SPW), np.float32)
    sp[:, SP_G1T:SP_G1T + 8] = norm1_g.reshape(8, 128).T
    sp[:, SP_G2T:SP_G2T + 8] = norm2_g.reshape(8, 128).T
    sp[:, SP_GV:SP_GV + 256] = a_vnorm_g[None, :]
    sp[:, SP_NG:SP_NG + 512] = c_norm_g[None, :]
    sp[:, SP_BST:SP_BST + 4] = a_bs.T
    cw = np.concatenate([conv_w, conv_b[None]], axis=0)
    sp[:, SP_CW:SP_CW + 176] = cw.reshape(4, 44, 128).transpose(2, 1, 0).reshape(128, 176)
    sp[:, SP_BSC:SP_BSC + 2] = b_scale.reshape(2, 2, 64).transpose(1, 2, 0).reshape(128, 2)
    return sp


def _constants(n_tiles):
    H, C = 8, 64
    lg = np.log1p(-np.exp2(-5.0 - np.arange(H, dtype=np.float64)))
    t = np.arange(128)
    same = (t[:, None] // 64) == (t[None, :] // 64)
    dm = np.exp(lg[None, :, None] * np.abs(t[:, None, None] - t[None, None, :])) * same[:, None, :] * 0.125
    qd = np.zeros((128, 4, 128))
    cd = np.zeros((128, 4))
    for r in range(2):
        for j in range(4):
            h = 2 * j + r
            qd[64 * r:64 * r + 64, j, :] = np.exp(lg[h] * ((t % 64) + 1))[None, :] * 0.125
            cd[64 * r:64 * r + 64, j] = np.exp(lg[h] * 64)
    kd = np.exp(lg[None, :] * (63 - (t % 64))[:, None])
    wins = (2, 4, 8, 16)
    pm = np.zeros((128, 3, 4, 128))
    for g, w in enumerate(wins):
        for tt in range(128):
            for ss in range(max(0, tt - w + 1), tt + 1):
                pm[ss, 0, g, tt] += 1.0 / w
            for ss in range(128 + tt - w + 1, 128):
                pm[ss, 1, g, tt] += 1.0 / w
            cnt = min(tt + 1, w)
            for ss in range(max(0, tt - w + 1), tt + 1):
                pm[ss, 2, g, tt] += 1.0 / cnt
            pm[tt, 0, g, tt] -= 1.0
            pm[tt, 2, g, tt] -= 1.0
    amask = ((t[None, :] // 64) >= (t[:, None] // 64)).astype(np.float64)
    cst = np.concatenate([dm.reshape(128, -1), qd.reshape(128, -1), cd, kd, amask], axis=1).astype(np.float32)
    cbf = np.concatenate([pm.reshape(128, -1), np.eye(128), pm[:, 2].reshape(128, -1), pm[:, 0].reshape(128, -1)],
                         axis=1).astype(np.float32)
    half = 32
    inv = 10000.0 ** (-np.arange(half, dtype=np.float64) / half)
    pos = np.arange(n_tiles * TILE, dtype=np.float64)
    ang = (pos.astype(np.float32)[:, None] * inv.astype(np.float32)[None, :]).astype(np.float32)
    cs = np.stack([np.cos(ang), np.sin(ang)], axis=1).astype(np.float32)
    rope = cs.reshape(n_tiles, 4, 128, 2, 32).transpose(0, 2, 3, 1, 4)
    return cst, cbf, np.ascontiguousarray(rope, dtype=np.float32)


CST_DM, CST_QD, CST_CD, CST_KD, CST_AM, CSTW = 0, 1024, 1536, 1540, 1548, 1676
CBF_PM, CBF_ID, CBF_PS0, CBF_PSL, CBFW = 0, 1536, 1664, 2176, 2688
GROUPS = [[0, 1], [2, 3], [4, 5], [6, 7]]


def build_program(n_tiles, layers, final_norm=True, lag=0, pipe=False):
    nc = bass.Bass("TRN2", target_bir_lowering=False)
    L = len(layers)
    ntok = n_tiles * TILE
    ROLE_d = nc.dram_tensor("ROLE", [128, 4], F32, kind="ExternalInput").ap()
    if pipe:
        stage_d = [nc.dram_tensor("stage%d" % i, [TILE, D], F32, kind="Internal").ap() for i in range(2)]
        gath_d = [nc.dram_tensor("gath%d" % i, [2 * TILE, D], F32, kind="Internal").ap() for i in range(2)]
    x_d = nc.dram_tensor("x", [ntok, D], F32, kind="ExternalInput").ap()
    out_d = nc.dram_tensor("out", [ntok, D], F32, kind="ExternalOutput").ap()
    W_d = [nc.dram_tensor("W%d" % l, [128, WCOLS], F32, kind="ExternalInput").ap() for l in range(L)]
    WB_d = [nc.dram_tensor("WB%d" % l, [128, 2048], F32, kind="ExternalInput").ap() for l in range(L)]
    SP_d = [nc.dram_tensor("SP%d" % l, [128, SPW], F32, kind="ExternalInput").ap() for l in range(L)]
    WS_d = [nc.dram_tensor("WS%d" % l, [128, 512], F32, kind="ExternalInput").ap() for l in range(L)]
    BW_d = [nc.dram_tensor("BW%d" % l, [64, 256], F32, kind="ExternalInput").ap() for l in range(L)]
    GF_d = nc.dram_tensor("GF", [128, D], F32, kind="ExternalInput").ap()
    CST_d = nc.dram_tensor("CST", [128, CSTW], F32, kind="ExternalInput").ap()
    CBF_d = nc.dram_tensor("CBF", [128, CBFW], F32, kind="ExternalInput").ap()
    ROPE_d = nc.dram_tensor("ROPE", [n_tiles, 128, 256], F32, kind="ExternalInput").ap()
    scr = {}
    for l in range(L):
        for (name, src, off, w, parts) in CHUNKS:
            scr[(l, name)] = nc.dram_tensor("scr_%d_%s" % (l, name), [parts, w], BF16, kind="Internal").ap()

    A = nc.alloc_sbuf_tensor
    xsb = [A("xs%d" % i, [128, 4, D], F32) for i in range(2)]
    cx = {"p": 0}

    def xs_():
        return xsb[cx["p"]]

    def xr(i):
        return ("x", cx["p"], i)
    ob = [A("ob%d" % i, [128, D], F32) for i in range(2)]
    hT = A("hT", [128, 8, TILE], BF16)
    hn = [A("hn%d" % i, [128, D], BF16) for i in range(2)]
    junks = [A("junk%d" % i, [128, D], BF16) for i in range(2)]
    jc = {"n": 0}

    def junk_():
        jc["n"] += 1
        k = jc["n"] % 2
        return junks[k], ("junk", k)
    st = A("st", [128, 64], F32)
    ring = [A("ring%d" % i, [128, SLOT], BF16) for i in range(NRING)]
    us = A("us", [128, 4, 256], F32)
    va = A("va", [128, 4, 256], BF16)
    Fs = [A("F%d" % i, [128, 512], F32) for i in range(6)]
    big = A("big", [128, NJ * TILE], BF16)
    actT = big[:, :].rearrange("p (j t) -> p j t", j=NJ)
    qrot = big[:, 0:2048].rearrange("p (i c) -> p i c", i=4)
    krot = big[:, 2048:4096].rearrange("p (i c) -> p i c", i=4)
    kd = big[:, 4096:6144].rearrange("p (i c) -> p i c", i=4)
    vs = big[:, 6144:8192].rearrange("p (i c) -> p i c", i=4)
    gz = big[:, 8192:10240].rearrange("p (i c) -> p i c", i=4)
    xb = [A("xb%d" % l, [128, 5, 256], BF16) for l in range(L)]
    qTs = [A("qT%d" % i, [128, 4, 128], BF16) for i in range(2)]
    qdTs = [A("qdT%d" % i, [128, 4, 128], BF16) for i in range(2)]
    kTs = [A("kT%d" % i, [128, 4, 128], BF16) for i in range(2)]
    scTs = [A("scT%d" % i, [128, 8, 128], BF16) for i in range(2)]
    S = [A("S%d" % l, [128, 4, 64], F32) for l in range(L)]
    Stmp = A("Stmp", [128, 4, 64], F32)
    Scar = [A("Scar%d" % l, [128, 8, 64], BF16) for l in range(L)]
    Sring = A("Sring", [128, 8, 8, 64], BF16)
    ycbs = [A("ycb%d" % i, [128, 512], BF16) for i in range(2)]
    yab = A("yab", [128, 256], BF16)
    ycT = A("ycT", [128, 4, TILE], BF16)
    yaT = A("yaT", [128, 2, TILE], BF16)
    ybT = A("ybT", [128, 2, TILE], BF16)
    plb = A("plb", [64, 512], BF16)
    halo = [A("halo%d" % l, [128, 44, 2], F32) for l in range(L)]
    hc = A("hc", [128, 44, 2], F32)
    hct = A("hct", [128, 44], F32)
    spm = [A("spm%d" % l, [128, SPW], F32) for l in range(L)]
    wmT = [A("wmT%d" % l, [128, 4, 128], BF16) for l in range(L)]
    wsf = A("wsf", [128, 4, 128], F32)
    bw = [A("bw%d" % l, [64, 4, 64], BF16) for l in range(L)]
    gf = A("gf", [128, D], F32)
    cst = A("cst", [128, CSTW], F32)
    cbf = A("cbf", [128, CBFW], BF16)
    rope = [A("rope%d" % i, [128, 2, 4, 32], F32) for i in range(2)]
    role = A("role", [128, 4], F32)
    xg = [A("xg%d" % i, [128, D], F32) for i in range(2)] if pipe else None

    psT = [nc.alloc_psum_tensor("psT%d" % i, [128, 1024], BF16) for i in range(2)]
    psF = [nc.alloc_psum_tensor("psF%d" % i, [128, 512], F32) for i in range(6)]

    P = Prog(nc)
    cnt = {"psF": 0, "psT": 0, "x": 0, "ob": 0, "hn": 0, "qk": 0, "st": 0, "cg": 0}

    def bank():
        i = cnt["psF"] % 6
        cnt["psF"] += 1
        return psF[i], ("psF", i)

    def tbank():
        i = cnt["psT"] % 2
        cnt["psT"] += 1
        return psT[i], ("psT", i)

    def stcol(n=1):
        c = cnt["st"]
        if c + n > 64:
            c = 0
        cnt["st"] = c + n
        return st[:, c:c + n], [("st", k) for k in range(c, c + n)]

    dm_v = cst[:, CST_DM:CST_DM + 1024].rearrange("p (h t) -> p h t", h=8)
    qd_v = cst[:, CST_QD:CST_QD + 512].rearrange("p (j t) -> p j t", j=4)
    cd_v = cst[:, CST_CD:CST_CD + 4]
    kd_v = cst[:, CST_KD:CST_KD + 8]
    am_v = cst[:, CST_AM:CST_AM + 128]
    pm_v = cbf[:, CBF_PM:CBF_PM + 1536].rearrange("p (a g t) -> p a g t", a=3, g=4)
    ps0_v = cbf[:, CBF_PS0:CBF_PS0 + 512].rearrange("p (g t) -> p g t", g=4)
    psl_v = cbf[:, CBF_PSL:CBF_PSL + 512].rearrange("p (g t) -> p g t", g=4)
    ident = cbf[:, CBF_ID:CBF_ID + 128]

    P.dma("sp", cst[:], CST_d, writes=["cst"], chan="cst")
    P.dma("pool", cbf[:], CBF_d, writes=["cbf"], chan="cbf")
    P.dma("sp", gf[:], GF_d, writes=["gf"], chan="gf")
    P.dma("sp", role[:], ROLE_d, writes=["role"], chan="role")
    for l in range(L):
        P.dma("sp", spm[l][:], SP_d[l], writes=[("spm", l)], chan=("spm", l))
        P.dma("pool", bw[l][:], BW_d[l].rearrange("p (g d) -> p g d", g=4), writes=[("bw", l)], chan=("bw", l))
        P.dma("sp", wsf[:], WS_d[l].rearrange("p (h t) -> p h t", h=4), writes=["wsf"], chan="wsf")
        P.dve(lambda e, l=l: e.tensor_tensor(out=wmT[l][:], in0=wsf[:],
                                             in1=am_v.unsqueeze(1).to_broadcast([128, 4, 128]), op=ALU.mult),
              reads=["wsf", "cst"], writes=[("wmT", l)])
        P.dve(lambda e, l=l: e.memset(S[l][:], 0.0), writes=[("S", l)])
        P.dve(lambda e, l=l: e.memset(Scar[l][:], 0.0), writes=[("Scar", l)])
        P.dve(lambda e, l=l: e.memset(halo[l][:], 0.0), writes=[("halo", l)])
        P.dve(lambda e, l=l: e.memset(xb[l][:, 0, :], 0.0), writes=[("xb", l, 0)])
    P.dve(lambda e: e.memset(Sring[:], 0.0), writes=[("Sring", n) for n in range(8)])

    seq = []
    for t in range(n_tiles):
        for l in range(L):
            for (name, src, off, w, parts) in CHUNKS:
                seq.append((l, name, w, parts))
    wst = {"issued": 0, "used": 0, "released": 0}

    def pump():
        sec, P.section = P.section, None
        while wst["issued"] < min(wst["released"] + NRING, len(seq)):
            i = wst["issued"]
            l, name, w, parts = seq[i]
            s = i % NRING
            if i < L * len(CHUNKS):
                cname, csrc, coff, cw_, cparts = CHUNKS[i % len(CHUNKS)]
                assert cname == name
                srcap = W_d[l][:, coff:coff + w] if csrc == "W" else WB_d[l][:, :]
                P.dma("pool", ring[s][0:parts, 0:w], srcap, writes=[("ring", s)], chan=("ringc", s))
                P.dma("sp", scr[(l, name)], ring[s][0:parts, 0:w], reads=[("ring", s)],
                      writes=[("scr", l, name)], chan=("scrw", i % 4))
            else:
                P.dma("sp", ring[s][0:parts, 0:w], scr[(l, name)], reads=[("scr", l, name)],
                      writes=[("ring", s)], chan=("ring", s))
            wst["issued"] += 1
        P.section = sec

    def use_chunk(l, name):
        i = wst["used"]
        assert seq[i][0] == l and seq[i][1] == name, (seq[i], l, name)
        pump()
        assert wst["issued"] > i
        wst["used"] += 1
        s = i % NRING
        return ring[s], ("ring", s)

    def release(n=1):
        wst["released"] += n
        pump()

    def rstd_from_ss(ss_ap, ss_res, n, scale):
        P.dve(lambda e: e.tensor_scalar(out=ss_ap, in0=ss_ap, scalar1=scale, scalar2=EPS,
                                        op0=ALU.mult, op1=ALU.add), reads=ss_res, writes=ss_res)
        P.dve(lambda e: e.reciprocal(out=ss_ap, in_=ss_ap), reads=ss_res, writes=ss_res)
        P.act(lambda e: e.activation(out=ss_ap, in_=ss_ap, func=AF.Sqrt), reads=ss_res, writes=ss_res)

    def norm_T(l, i, gcol):
        xi = xs_()[:, i, :]
        ss, ssr = stcol()
        P.dve(lambda e: e.memset(ss, 0.0), writes=ssr)
        jk, jkr = junk_()
        P.act(lambda e: e.activation(out=jk[:], in_=xi, func=AF.Square, accum_out=ss),
              reads=[xr(i)] + ssr, writes=[jkr] + ssr)
        rstd_from_ss(ss, ssr, 1, 1.0 / D)
        hb = cnt["hn"] % 2
        cnt["hn"] += 1
        P.act(lambda e: e.activation(out=hn[hb][:], in_=xi, func=AF.Copy, scale=ss),
              reads=[xr(i)] + ssr, writes=[("hn", hb)])
        pt, ptr = tbank()
        for k in range(8):
            P.pe(lambda e, k=k: e.transpose(out=pt[:, k * 128:(k + 1) * 128], in_=hn[hb][:, k * 128:(k + 1) * 128],
                                            identity=ident),
                 reads=[("hn", hb), "cbf"], writes=[ptr], sig=(k == 7))
        gb = spm[l][:, gcol:gcol + 8].unsqueeze(2).to_broadcast([128, 8, 128])
        P.dve(lambda e: e.tensor_tensor(out=hT[:, :, i * 128:(i + 1) * 128],
                                        in0=pt[:].rearrange("p (k t) -> p k t", k=8), in1=gb, op=ALU.mult),
              reads=[ptr, ("spm", l)], writes=[("hT", i)])

    def transpose_to(src_ap, src_res, nblk, dst_fn, dst_res, eng="act"):
        pt, ptr = tbank()
        for j in range(nblk):
            P.pe(lambda e, j=j: e.transpose(out=pt[:, j * 128:(j + 1) * 128], in_=src_ap[:, j * 128:(j + 1) * 128],
                                            identity=ident),
                 reads=list(src_res) + ["cbf"], writes=[ptr], sig=(j == nblk - 1))
        src_v = pt[:, 0:nblk * 128].rearrange("p (j t) -> p j t", j=nblk)
        if eng == "act":
            P.act(lambda e: e.activation(out=dst_fn, in_=src_v, func=AF.Copy), reads=[ptr], writes=dst_res)
        else:
            P.dve(lambda e: e.tensor_copy(out=dst_fn, in_=src_v), reads=[ptr], writes=dst_res)
        return pt, ptr

    def rotary(l, i, pb, pbr, dst, dst_res, tix):
        rb = tix % 2
        f = cnt["qk"] % 2
        cnt["qk"] += 1
        qf, qfr = Fs[f], ("F", f)
        ta, tar = Fs[2 + f], ("F", 2 + f)
        tb, tbr = Fs[4 + f], ("F", 4 + f)
        P.act(lambda e: e.activation(out=qf[:], in_=pb[:], func=AF.Copy), reads=[pbr], writes=[qfr])
        cosb = rope[rb][:, 0, i, :].unsqueeze(1).unsqueeze(1).to_broadcast([128, 8, 2, 32])
        sinb = rope[rb][:, 1, i, :].unsqueeze(1).to_broadcast([128, 8, 32])
        qv = qf[:].rearrange("p (h a e) -> p h a e", h=8, a=2)
        tAv = ta[:].rearrange("p (h a e) -> p h a e", h=8, a=2)
        tBv = tb[:].rearrange("p (h a e) -> p h a e", h=8, a=2)
        P.dve(lambda e: e.tensor_tensor(out=tAv, in0=qv, in1=cosb, op=ALU.mult),
              reads=[qfr, ("rope", rb)], writes=[tar])
        P.dve(lambda e: e.scalar_tensor_tensor(out=tBv[:, :, 0, :], in0=qv[:, :, 1, :], scalar=-1.0, in1=sinb,
                                               op0=ALU.mult, op1=ALU.mult),
              reads=[qfr, ("rope", rb)], writes=[tbr])
        P.dve(lambda e: e.tensor_tensor(out=tBv[:, :, 1, :], in0=qv[:, :, 0, :], in1=sinb, op=ALU.mult),
              reads=[qfr, ("rope", rb)], writes=[tbr])
        P.dve(lambda e: e.tensor_tensor(out=dst, in0=ta[:], in1=tb[:], op=ALU.add),
              reads=[tar, tbr], writes=dst_res)

    def emit_norm1(l):
        for i in range(4):
            norm_T(l, i, SP_G1T)

    def emit_layer(l, tix, pre_normed=False, mid_hook=None, tail_hook=None):
        sp_ = spm[l]
        if not pre_normed:
            emit_norm1(l)
        hT_res = [("hT", i) for i in range(4)]
        for gi in range(6):
            slot, sres = use_chunk(l, "win%d" % gi)
            ncol = 512 if gi < 5 else 256
            wv = slot[:, 0:8 * ncol].rearrange("p (k c) -> p k c", k=8)
            for i in range(4):
                pb, pbr = bank()
                for k in range(8):
                    P.pe(lambda e, k=k, i=i, pb=pb, wv=wv, ncol=ncol: e.matmul(
                        pb[:, 0:ncol], lhsT=hT[:, k, i * 128:(i + 1) * 128], rhs=wv[:, k, :],
                        start=(k == 0), stop=(k == 7)),
                        reads=[("hT", i), sres], writes=[pbr], sig=(k == 7))
                if gi == 0:
                    f = cnt["qk"] % 2
                    cnt["qk"] += 1
                    vt, vtr = Fs[f], ("F", f)
                    P.act(lambda e: e.activation(out=us[:, i, :], in_=pb[:, 0:256], func=AF.Gelu_apprx_tanh),
                          reads=[pbr], writes=[("us", i)])
                    P.act(lambda e: e.activation(out=vt[:, 0:256], in_=pb[:, 256:512], func=AF.Gelu_apprx_tanh),
                          reads=[pbr], writes=[vtr])
                    ss, ssr = stcol()
                    P.dve(lambda e: e.memset(ss, 0.0), writes=ssr)
                    jk, jkr = junk_()
                    P.act(lambda e: e.activation(out=jk[:, 0:256], in_=vt[:, 0:256], func=AF.Square, accum_out=ss),
                          reads=[vtr] + ssr, writes=[jkr] + ssr)
                    rstd_from_ss(ss, ssr, 1, 1.0 / 256)
                    P.dve(lambda e: e.scalar_tensor_tensor(
                        out=va[:, i, :], in0=vt[:, 0:256], scalar=ss, in1=sp_[:, SP_GV:SP_GV + 256],
                        op0=ALU.mult, op1=ALU.mult),
                        reads=[vtr, ("spm", l)] + ssr, writes=[("va", i)])
                elif gi == 1:
                    rotary(l, i, pb, pbr, qrot[:, i, :], [("big", i)], tix)
                elif gi == 2:
                    rotary(l, i, pb, pbr, krot[:, i, :], [("big", 4 + i)], tix)
                    P.dve(lambda e, i=i: e.tensor_tensor(
                        out=kd[:, i, :].rearrange("p (h d) -> p h d", h=8),
                        in0=krot[:, i, :].rearrange("p (h d) -> p h d", h=8),
                        in1=kd_v.unsqueeze(2).to_broadcast([128, 8, 64]), op=ALU.mult),
                        reads=[("big", 4 + i), "cst"], writes=[("big", 8 + i)])
                elif gi == 3:
                    P.act(lambda e, i=i, pb=pb: e.activation(out=vs[:, i, :], in_=pb[:], func=AF.Copy),
                          reads=[pbr], writes=[("big", 12 + i)])
                elif gi == 4:
                    P.act(lambda e, i=i, pb=pb: e.activation(out=gz[:, i, :], in_=pb[:], func=AF.Silu),
                          reads=[pbr], writes=[("big", 16 + i)])
                    P.pool(lambda e, i=i: e.tensor_tensor(out=gz[:, i, :], in0=gz[:, i, :],
                                                          in1=sp_[:, SP_NG:SP_NG + 512], op=ALU.mult),
                           reads=[("big", 16 + i), ("spm", l)], writes=[("big", 16 + i)])
                else:
                    P.act(lambda e, i=i, pb=pb: e.activation(out=xb[l][:, 1 + i, :], in_=pb[:, 0:256], func=AF.Copy),
                          reads=[pbr], writes=[("xb", l, 1 + i)])
            release()
        if mid_hook is not None:
            mid_hook()
        wa0, wa0r = use_chunk(l, "wouta0")
        wa1, wa1r = use_chunk(l, "wouta1")
        wbb, wbbr = use_chunk(l, "woutb")
        wa0v = wa0[:, :].rearrange("p (c n) -> p c n", c=4)
        wa1v = wa1[:, 0:2048].rearrange("p (c n) -> p c n", c=2)
        wbv = wbb[:, 0:2048].rearrange("p (c n) -> p c n", c=2)
        def stage_a(i):
            ycb = ycbs[i % 2]
            ip = i % 2
            qT, qdT, kT, scT = qTs[ip], qdTs[ip], kTs[ip], scTs[ip]
            P.section = "ret"
            pt_q, ptr_q = transpose_to(qrot[:, i, :], [("big", i)], 4, qT[:], [("qT", ip)], eng="act")
            P.dve(lambda e, pt_q=pt_q: e.tensor_tensor(out=qdT[:], in0=pt_q[:, 0:512].rearrange("p (j t) -> p j t", j=4),
                                                      in1=qd_v, op=ALU.mult),
                  reads=[ptr_q, "cst"], writes=[("qdT", ip)])
            transpose_to(krot[:, i, :], [("big", 4 + i)], 4, kT[:], [("kT", ip)], eng="act")
            P.section = "ret_sc"
            b0, b0r = bank()
            b1, b1r = bank()
            bs_, bsr = (b0, b1), (b0r, b1r)
            for h in range(8):
                j, r = h // 2, h % 2
                P.pe(lambda e, j=j, r=r: e.matmul(bs_[r][:, j * 128:(j + 1) * 128],
                                                  lhsT=kT[64 * r:64 * r + 64, j, :], rhs=qT[64 * r:64 * r + 64, j, :],
                                                  start=True, stop=True, tile_position=(64 * r, 0)),
                     reads=[("kT", ip), ("qT", ip)], writes=[bsr[r]], sig=(h >= 6))
            for r in range(2):
                P.dve(lambda e, r=r: e.tensor_tensor(out=scT[:, r::2, :],
                                                     in0=bs_[r][:].rearrange("p (j t) -> p j t", j=4),
                                                     in1=dm_v[:, r::2, :], op=ALU.mult),
                      reads=[bsr[r], "cst"], writes=[("scT", ip, r)])
            P.section = "ret_kv"
            k0, k0r = bank()
            k1, k1r = bank()
            kb, kbr = (k0, k1), (k0r, k1r)
            for c in range(2):
                for j in range(4):
                    P.pe(lambda e, c=c, j=j: e.matmul(
                        kb[c][:, j * 128:(j + 1) * 128],
                        lhsT=kd[64 * c:64 * c + 64, i, j * 128:(j + 1) * 128],
                        rhs=vs[64 * c:64 * c + 64, i, j * 128:(j + 1) * 128],
                        start=True, stop=True, tile_position=(64 * c, 0)),
                        reads=[("big", 8 + i), ("big", 12 + i)], writes=[kbr[c]], sig=(j == 3))
            for c in range(2):
                n = 2 * i + c
                P.dve(lambda e: e.tensor_tensor(out=Stmp[:], in0=S[l][:],
                                                in1=cd_v.unsqueeze(2).to_broadcast([128, 4, 64]), op=ALU.mult),
                      reads=[("S", l), "cst"], writes=["Stmp"])
                for r in range(2):
                    P.dve(lambda e, c=c, r=r: e.tensor_tensor(
                        out=S[l][64 * r:64 * r + 64, :, :], in0=Stmp[64 * r:64 * r + 64, :, :],
                        in1=kb[c][64 * r:64 * r + 64, :].rearrange("p (j a e) -> p j a e", j=4, a=2)[:, :, r, :],
                        op=ALU.add),
                        reads=["Stmp", kbr[c]], writes=[("S", l)])
                for r in range(2):
                    P.act(lambda e, n=n, r=r: e.activation(out=Sring[64 * r:64 * r + 64, n, r::2, :],
                                                           in_=S[l][64 * r:64 * r + 64, :, :], func=AF.Copy),
                          reads=[("S", l)], writes=[("Sring", n)])
            P.section = "ret_y"
            yb_, ybr = bank()
            for h in range(8):
                j, r = h // 2, h % 2
                P.pe(lambda e, h=h: e.matmul(yb_[:, h * 64:(h + 1) * 64], lhsT=scT[:, h, :],
                                             rhs=vs[:, i, h * 64:(h + 1) * 64], start=True, stop=False),
                     reads=[("scT", ip, r), ("big", 12 + i)], writes=[ybr], sig=False)
                for c in range(2):
                    n = 2 * i + c
                    if n == 0 and L == 1:
                        sst, sstr = Sring[:, 7, h, :], ("Sring", 7)
                    elif n == 0:
                        sst, sstr = Scar[l][:, h, :], ("Scar", l)
                    else:
                        sst, sstr = Sring[:, n - 1, h, :], ("Sring", n - 1)
                    P.pe(lambda e, h=h, j=j, c=c, sst=sst: e.matmul(
                        yb_[64 * c:64 * c + 64, h * 64:(h + 1) * 64], lhsT=qdT[:, j, 64 * c:64 * c + 64], rhs=sst,
                        start=False, stop=True, tile_position=(0, 64 * c)),
                        reads=[("qdT", ip), sstr], writes=[ybr], sig=(h == 7 and c == 1))
            P.section = "ret_gn"
            yv = yb_[:].rearrange("p (h e) -> p h e", h=8)
            s1, s1r = stcol(8)
            s2, s2r = stcol(8)
            mq, mqr = stcol(8)
            P.dve(lambda e: e.tensor_reduce(out=s1, in_=yv, axis=AX.X, op=ALU.add), reads=[ybr], writes=s1r)
            ysq, yt1, yt2, yat = Fs[0], Fs[1], Fs[2], Fs[3]
            P.act(lambda e: e.activation(out=ysq[:], in_=yb_[:], func=AF.Square), reads=[ybr], writes=[("F", 0)])
            P.dve(lambda e: e.tensor_reduce(out=s2, in_=ysq[:].rearrange("p (h e) -> p h e", h=8), axis=AX.X,
                                            op=ALU.add), reads=[("F", 0)], writes=s2r)
            P.dve(lambda e: e.tensor_scalar(out=s1, in0=s1, scalar1=1.0 / 64, scalar2=None, op0=ALU.mult),
                  reads=s1r, writes=s1r)
            P.dve(lambda e: e.tensor_tensor(out=mq, in0=s1, in1=s1, op=ALU.mult), reads=s1r, writes=mqr)
            P.dve(lambda e: e.scalar_tensor_tensor(out=s2, in0=s2, scalar=1.0 / 64, in1=mq, op0=ALU.mult,
                                                   op1=ALU.subtract), reads=s2r + mqr, writes=s2r)
            P.dve(lambda e: e.tensor_scalar(out=s2, in0=s2, scalar1=EPS, scalar2=None, op0=ALU.add),
                  reads=s2r, writes=s2r)
            P.dve(lambda e: e.reciprocal(out=s2, in_=s2), reads=s2r, writes=s2r)
            P.act(lambda e: e.activation(out=s2, in_=s2, func=AF.Sqrt), reads=s2r, writes=s2r)
            P.dve(lambda e: e.tensor_tensor(out=yt1[:].rearrange("p (h e) -> p h e", h=8), in0=yv,
                                            in1=s1.unsqueeze(2).to_broadcast([128, 8, 64]), op=ALU.subtract),
                  reads=[ybr] + s1r, writes=[("F", 1)])
            P.dve(lambda e: e.tensor_tensor(out=yt2[:].rearrange("p (h e) -> p h e", h=8),
                                            in0=yt1[:].rearrange("p (h e) -> p h e", h=8),
                                            in1=s2.unsqueeze(2).to_broadcast([128, 8, 64]), op=ALU.mult),
                  reads=[("F", 1)] + s2r, writes=[("F", 2)])
            P.dve(lambda e: e.tensor_tensor(out=ycb[:], in0=yt2[:], in1=gz[:, i, :], op=ALU.mult),
                  reads=[("F", 2), ("big", 16 + i)], writes=[("ycb", i % 2)])

        def stage_b(i):
            ycb = ycbs[i % 2]
            P.section = "ret_gn"
            transpose_to(ycb[:], [("ycb", i % 2)], 4, ycT[:, :, i * 128:(i + 1) * 128], [("ycT", i)], eng="act")
            P.section = "A"
            yat = Fs[3]
            ab, abr = bank()
            for h in range(4):
                P.pe(lambda e, h=h: e.matmul(ab[:, h * 64:(h + 1) * 64], lhsT=wmT[l][:, h, :],
                                             rhs=va[:, i, h * 64:(h + 1) * 64], start=True, stop=True),
                     reads=[("wmT", l), ("va", i)], writes=[abr], sig=(h == 3))
            P.dve(lambda e: e.tensor_tensor(out=yat[:, 0:256].rearrange("p (h d) -> p h d", h=4),
                                            in0=ab[:, 0:256].rearrange("p (h d) -> p h d", h=4),
                                            in1=sp_[:, SP_BST:SP_BST + 4].unsqueeze(2).to_broadcast([128, 4, 64]),
                                            op=ALU.add),
                  reads=[abr, ("spm", l)], writes=[("F", 3)])
            P.dve(lambda e: e.tensor_tensor(out=yab[:], in0=yat[:, 0:256], in1=us[:, i, :], op=ALU.mult),
                  reads=[("F", 3), ("us", i)], writes=["yab"])
            transpose_to(yab[:], ["yab"], 2, yaT[:, :, i * 128:(i + 1) * 128], [("yaT", i)], eng="act")
            P.section = "B"
            pbk, pbkr = bank()
            if i == 0 and tix == 0:
                pdiag = ps0_v
            elif i == 0 and tix == lag and lag > 0:
                pdiag = psl_v
            else:
                pdiag = pm_v[:, 0]
            for g in range(4):
                P.pe(lambda e, g=g: e.matmul(pbk[0:64, g * 128:(g + 1) * 128],
                                             lhsT=xb[l][:, 1 + i, g * 64:(g + 1) * 64],
                                             rhs=pdiag[:, g, :], start=True, stop=False),
                     reads=[("xb", l, 1 + i), "cbf"], writes=[pbkr], sig=False)
                P.pe(lambda e, g=g: e.matmul(pbk[0:64, g * 128:(g + 1) * 128],
                                             lhsT=xb[l][:, i, g * 64:(g + 1) * 64],
                                             rhs=pm_v[:, 1, g, :], start=False, stop=True),
                     reads=[("xb", l, i), "cbf"], writes=[pbkr], sig=(g == 3))
            P.act(lambda e: e.activation(out=plb[:], in_=pbk[0:64, :], func=AF.Copy), reads=[pbkr], writes=["plb"])
            pb2, pb2r = bank()
            for g in range(4):
                hf, pr = g % 2, g // 2
                P.pe(lambda e, g=g, hf=hf, pr=pr: e.matmul(
                    pb2[64 * hf:64 * hf + 64, pr * 128:(pr + 1) * 128], lhsT=bw[l][:, g, :],
                    rhs=plb[:, g * 128:(g + 1) * 128], start=True, stop=True, tile_position=(0, 64 * hf)),
                    reads=[("bw", l), "plb"], writes=[pb2r], sig=(g == 3))
            P.dve(lambda e: e.tensor_tensor(out=ybT[:, :, i * 128:(i + 1) * 128],
                                            in0=pb2[:, 0:256].rearrange("p (g t) -> p g t", g=2),
                                            in1=sp_[:, SP_BSC:SP_BSC + 2].unsqueeze(2).to_broadcast([128, 2, 128]),
                                            op=ALU.mult),
                  reads=[pb2r, ("spm", l)], writes=[("ybT", i)])
            P.section = "wo"
            for n in range(2):
                ob_, obr = bank()
                mm = []
                for c in range(2):
                    mm.append((yaT[:, c, i * 128:(i + 1) * 128], wa0v[:, c, n * 512:(n + 1) * 512], ("yaT", i), wa0r))
                for c in range(2):
                    mm.append((ycT[:, c, i * 128:(i + 1) * 128], wa0v[:, 2 + c, n * 512:(n + 1) * 512], ("ycT", i), wa0r))
                for c in range(2):
                    mm.append((ycT[:, 2 + c, i * 128:(i + 1) * 128], wa1v[:, c, n * 512:(n + 1) * 512], ("ycT", i), wa1r))
                for g in range(2):
                    mm.append((ybT[:, g, i * 128:(i + 1) * 128], wbv[:, g, n * 512:(n + 1) * 512], ("ybT", i), wbbr))
                for q, (lh, rh, r1, r2) in enumerate(mm):
                    P.pe(lambda e, lh=lh, rh=rh, q=q: e.matmul(ob_[:], lhsT=lh, rhs=rh, start=(q == 0),
                                                               stop=(q == len(mm) - 1)),
                         reads=[r1, r2], writes=[obr], sig=(q == len(mm) - 1))
                P.dve(lambda e, n=n: e.tensor_tensor(out=xs_()[:, i, n * 512:(n + 1) * 512],
                                                     in0=xs_()[:, i, n * 512:(n + 1) * 512], in1=ob_[:], op=ALU.add),
                      reads=[xr(i), obr], writes=[xr(i)])

        for (st_, i_) in (("a", 0), ("a", 1), ("b", 0), ("a", 2), ("b", 1), ("a", 3), ("b", 2), ("b", 3)):
            (stage_a if st_ == "a" else stage_b)(i_)
        P.section = None
        release(3)
        P.dve(lambda e: e.tensor_copy(out=xb[l][:, 0, :], in_=xb[l][:, 4, :]),
              reads=[("xb", l, 4)], writes=[("xb", l, 0)])
        if L > 1:
            P.pool(lambda e: e.tensor_copy(out=Scar[l][:], in_=Sring[:, 7, :, :]),
                   reads=[("Sring", 7)], writes=[("Scar", l)])
        P.section = "ffn"
        for i in range(4):
            norm_T(l, i, SP_G2T)
        cwv = sp_[:, SP_CW:SP_CW + 176].rearrange("p (j f) -> p j f", f=4)
        P.dve(lambda e: e.tensor_tensor(out=hct[:], in0=halo[l][:, :, 1], in1=cwv[:, :, 1], op=ALU.mult),
              reads=[("halo", l), ("spm", l)], writes=["hct"])
        P.dve(lambda e: e.tensor_tensor(out=hc[:, :, 0], in0=halo[l][:, :, 0], in1=cwv[:, :, 0], op=ALU.mult),
              reads=[("halo", l), ("spm", l)], writes=[("hc", 0)])
        P.dve(lambda e: e.tensor_tensor(out=hc[:, :, 0], in0=hc[:, :, 0], in1=hct[:], op=ALU.add),
              reads=[("hc", 0), "hct"], writes=[("hc", 0)])
        P.dve(lambda e: e.tensor_tensor(out=hc[:, :, 1], in0=halo[l][:, :, 1], in1=cwv[:, :, 0], op=ALU.mult),
              reads=[("halo", l), ("spm", l)], writes=[("hc", 1)])
        for c in range(11):
            slot, sres = use_chunk(l, "wup%d" % c)
            wv = slot[:, :].rearrange("p (a k f) -> p a k f", a=2, k=8)
            for a in range(2):
                jj = 2 * c + a
                f = cnt["cg"] % 3
                cnt["cg"] += 1
                res = []
                pbs = []
                for gv in range(2):
                    pb, pbr = bank()
                    halves = ((0, 256), (256, 512)) if jj < 2 else ((0, 512),)
                    for (t0_, t1_) in halves:
                        hres = [("hT", q) for q in range(t0_ // 128, t1_ // 128)]
                        for k in range(8):
                            P.pe(lambda e, k=k, a=a, gv=gv, pb=pb, wv=wv, t0_=t0_, t1_=t1_: e.matmul(
                                pb[:, t0_:t1_], lhsT=wv[:, a, k, gv * 128:(gv + 1) * 128], rhs=hT[:, k, t0_:t1_],
                                start=(k == 0), stop=(k == 7)),
                                reads=hres + [sres], writes=[pbr], sig=(k == 7))
                    pbs.append((pb, pbr, jj + 22 * gv, Fs[2 * f + gv], ("F", 2 * f + gv)))
                    res.append(("F", 2 * f + gv))
                for (pb, pbr, jg, cb, cbr) in pbs:
                    P.act(lambda e, pb=pb, cb=cb, jg=jg: e.activation(out=cb[:], in_=pb[:], func=AF.Identity,
                                                                      scale=cwv[:, jg, 2:3], bias=cwv[:, jg, 3:4]),
                          reads=[pbr, ("spm", l)], writes=[cbr])
                for (pb, pbr, jg, cb, cbr) in pbs:
                    P.dve(lambda e, pb=pb, cb=cb, jg=jg: e.scalar_tensor_tensor(
                        out=cb[:, 1:TILE], in0=pb[:, 0:TILE - 1], scalar=cwv[:, jg, 1:2], in1=cb[:, 1:TILE],
                        op0=ALU.mult, op1=ALU.add), reads=[pbr, cbr, ("spm", l)], writes=[cbr])
                for (pb, pbr, jg, cb, cbr) in pbs:
                    P.dve(lambda e, pb=pb, cb=cb, jg=jg: e.scalar_tensor_tensor(
                        out=cb[:, 2:TILE], in0=pb[:, 0:TILE - 2], scalar=cwv[:, jg, 0:1], in1=cb[:, 2:TILE],
                        op0=ALU.mult, op1=ALU.add), reads=[pbr, cbr, ("spm", l)], writes=[cbr])
                for (pb, pbr, jg, cb, cbr) in pbs:
                    P.act(lambda e, pb=pb, jg=jg: e.activation(out=halo[l][:, jg, :], in_=pb[:, TILE - 2:TILE],
                                                               func=AF.Copy),
                          reads=[pbr], writes=[("halo", l)])
                for (pb, pbr, jg, cb, cbr) in pbs:
                    P.pool(lambda e, cb=cb, jg=jg: e.tensor_tensor(out=cb[:, 0:2], in0=cb[:, 0:2], in1=hc[:, jg, :],
                                                                   op=ALU.add),
                           reads=[cbr, ("hc", 0), ("hc", 1)], writes=[cbr])
                P.act(lambda e, f=f: e.activation(out=Fs[2 * f][:], in_=Fs[2 * f][:], func=AF.Silu),
                      reads=[res[0]], writes=[res[0]])
                P.pool(lambda e, f=f, jj=jj: e.tensor_tensor(out=actT[:, jj, :], in0=Fs[2 * f][:], in1=Fs[2 * f + 1][:],
                                                             op=ALU.mult),
                       reads=res, writes=[("big", jj)])
            release()
        for n in range(2):
            accs = [bank() for _ in range(4)]
            for c in range(3):
                slot, sres = use_chunk(l, "wdn%d_%d" % (n, c))
                nj = 8 if c < 2 else 6
                wv = slot[:, 0:nj * 512].rearrange("p (j f) -> p j f", j=nj)
                for i in range(4):
                    for q in range(nj):
                        jj = 8 * c + q
                        P.pe(lambda e, i=i, q=q, jj=jj, wv=wv: e.matmul(
                            accs[i][0][:], lhsT=actT[:, jj, i * 128:(i + 1) * 128], rhs=wv[:, q, :],
                            start=(jj == 0), stop=(jj == NJ - 1)),
                            reads=[("big", jj), sres], writes=[accs[i][1]], sig=(q == nj - 1))
                release()
            if n == 1 and tail_hook is not None:
                P.section = None
                tail_hook()
                P.section = "ffn"
            for i in range(4):
                P.dve(lambda e, i=i, n=n: e.tensor_tensor(out=xs_()[:, i, n * 512:(n + 1) * 512],
                                                          in0=xs_()[:, i, n * 512:(n + 1) * 512], in1=accs[i][0][:],
                                                          op=ALU.add),
                      reads=[xr(i), accs[i][1]], writes=[xr(i)])
        P.section = None

    P.section = None
    out_chans = []
    def load_input(tix):
        par = tix % 2
        xb_ = xsb[par]
        P.dma("pool", xb_[:], x_d[tix * TILE:(tix + 1) * TILE, :].rearrange("(i p) d -> p i d", p=128),
              writes=[("x", par, i) for i in range(4)], chan=("xload", par))
        rb = tix % 2
        P.dma("sp", rope[rb][:], ROPE_d[tix].rearrange("p (a i e) -> p a i e", a=2, i=4),
              writes=[("rope", rb)], chan=("rope", rb))

    def load_sel(tix):
        par = tix % 2
        xb_ = xsb[par]
        if pipe and tix >= lag:
            gp = (tix - lag) % 2
            for i in range(4):
                gb = cnt["x"] % 2
                cnt["x"] += 1
                P.dma("pool", xg[gb][:], gath_d[gp][i * 128:(i + 1) * 128, :], reads=[("gath", gp)],
                      writes=[("xg", gb)], chan=("xg", gb))
                P.dve(lambda e, i=i, gb=gb: e.scalar_tensor_tensor(out=xb_[:, i, :], in0=xg[gb][:], scalar=role[:, 1:2],
                                                                   in1=xb_[:, i, :], op0=ALU.mult, op1=ALU.add),
                      reads=[("x", par, i), ("xg", gb), "role"], writes=[("x", par, i)])

    load_input(0)
    load_sel(0)
    cx["p"] = 0
    emit_norm1(0)
    for tix in range(n_tiles):
        cx["p"] = tix % 2
        has_next = tix + 1 < n_tiles

        def mid_hook(tix=tix):
            if tix + 1 < n_tiles:
                load_input(tix + 1)

        def tail_hook(tix=tix):
            if tix + 1 < n_tiles:
                load_sel(tix + 1)
                cx["p"] = (tix + 1) % 2
                emit_norm1(0)
                cx["p"] = tix % 2

        for l in range(L):
            emit_layer(l, tix, pre_normed=(l == 0), mid_hook=(mid_hook if l == 0 else None),
                       tail_hook=(tail_hook if l == L - 1 else None))
        sp_ = tix % 2
        for i in range(4):
            o = cnt["ob"] % 2
            cnt["ob"] += 1
            if final_norm:
                ss, ssr = stcol()
                P.dve(lambda e, ss=ss: e.memset(ss, 0.0), writes=ssr)
                jk, jkr = junk_()
                P.act(lambda e, i=i, ss=ss: e.activation(out=jk[:], in_=xs_()[:, i, :], func=AF.Square, accum_out=ss),
                      reads=[xr(i)] + ssr, writes=[jkr] + ssr)
                rstd_from_ss(ss, ssr, 1, 1.0 / D)
                P.dve(lambda e, ss=ss: e.tensor_scalar(out=ss, in0=ss, scalar1=role[:, 2:3], scalar2=role[:, 3:4],
                                                       op0=ALU.mult, op1=ALU.add),
                      reads=ssr + ["role"], writes=ssr)
                P.dve(lambda e, i=i, ss=ss, o=o: e.scalar_tensor_tensor(out=ob[o][:], in0=xs_()[:, i, :], scalar=ss,
                                                                        in1=gf[:], op0=ALU.mult, op1=ALU.mult),
                      reads=[xr(i), "gf", "role"] + ssr, writes=[("ob", o)])
            else:
                P.act(lambda e, i=i, o=o: e.activation(out=ob[o][:], in_=xs_()[:, i, :], func=AF.Copy),
                      reads=[xr(i)], writes=[("ob", o)])
            r0 = tix * TILE + i * 128
            if pipe and tix < n_tiles - lag:
                P.dma("pool", stage_d[sp_][i * 128:(i + 1) * 128, :], ob[o][:], reads=[("ob", o)],
                      writes=[("stage", sp_, i)], chan=("sst", o))
            P.dma("pool", out_d[r0:r0 + 128, :], ob[o][:], reads=[("ob", o)], chan=("ost", o))
            if ("ost", o) not in out_chans:
                out_chans.append(("ost", o))
        if pipe and tix < n_tiles - lag:
            P.async_op("pool", lambda e: e.collective_compute("AllGather", ALU.bypass, replica_groups=GROUPS,
                                                              ins=[stage_d[sp_]], outs=[gath_d[sp_]]),
                       reads=[("stage", sp_, i) for i in range(4)], writes=[("gath", sp_)], chan=("ag", sp_), inc=1)
    assert wst["used"] == len(seq)
    print("[build] sbuf bytes remaining per partition:", nc.sbuf_bytes_remaining, "ops:", len(P.ops))
    P.finalize(final_wait_chans=out_chans)
    return nc, len(P.ops)


def _layer_inputs(slot, li, p):
    W, WB = _prep_layer(p["w_in"][li], p["w_out"][li], p["w_up"][li], p["w_down"][li])
    SPm = _prep_small(p["norm1_g"][li], p["norm2_g"][li], p["a_vnorm_g"][li], p["c_norm_g"][li], p["a_bs"][li],
                      p["conv_w"][li], p["conv_b"][li], p["b_scale"][li])
    WS = np.ascontiguousarray(p["a_ws"][li].transpose(2, 0, 1).reshape(128, 512), dtype=np.float32)
    BW = np.ascontiguousarray(p["b_w"][li].transpose(1, 0, 2).reshape(64, 256), dtype=np.float32)
    return {"W%d" % slot: W, "WB%d" % slot: WB, "SP%d" % slot: SPm, "WS%d" % slot: WS, "BW%d" % slot: BW}


_PROG_CACHE = {}
LAG = 2


def _get_prog(key, *args, **kw):
    if key not in _PROG_CACHE:
        _PROG_CACHE[key] = build_program(*args, **kw)[0]
    return _PROG_CACHE[key]


def run_model(x, p, mode="pipe"):
    B, S, _ = x.shape
    depth = p["w_in"].shape[0]
    n_tiles = S // TILE
    cst, cbf, rope = _constants(n_tiles)
    rope = rope.reshape(n_tiles, 128, 256)
    GF = np.ascontiguousarray(np.broadcast_to(p["final_g"][None, :], (128, D)), dtype=np.float32)
    role_last = np.tile(np.array([1, 0, 1, 0], np.float32)[None, :], (128, 1))
    if mode == "pipe":
        assert depth == 2 and 2 * B == 8
        n_steps = n_tiles + LAG
        nc = _get_prog(("pipe", n_tiles), n_steps, [0], final_norm=True, lag=LAG, pipe=True)
        cbf1 = cbf.copy()
        cbf1[:, CBF_PS0:CBF_PS0 + 512] = cbf[:, CBF_PSL:CBF_PSL + 512]
        cbf1[:, CBF_PSL:CBF_PSL + 512] = cbf[:, CBF_PS0:CBF_PS0 + 512]
        zt = np.zeros((LAG, 128, 256), np.float32)
        per_role = [
            dict(GF=np.ones((128, D), np.float32), CST=cst, CBF=cbf, ROPE=np.concatenate([rope, zt], axis=0),
                 ROLE=np.tile(np.array([1, 0, 0, 1], np.float32)[None, :], (128, 1))),
            dict(GF=GF, CST=cst, CBF=cbf1, ROPE=np.concatenate([zt, rope], axis=0),
                 ROLE=np.tile(np.array([0, 1, 1, 0], np.float32)[None, :], (128, 1))),
        ]
        for r in range(2):
            per_role[r].update(_layer_inputs(0, r, p))
        zx = np.zeros((n_steps * TILE, D), np.float32)
        in_maps = []
        for c in range(2 * B):
            b, r = c // 2, c % 2
            m = dict(per_role[r])
            if r == 0:
                m["x"] = np.concatenate([x[b], np.zeros((LAG * TILE, D), np.float32)], axis=0)
            else:
                m["x"] = zx
            in_maps.append(m)
        res = run_bass_kernel_spmd(nc, in_maps, core_ids=list(range(2 * B)))
        return np.stack([np.asarray(res.results[2 * b + 1]["out"], dtype=np.float32)[LAG * TILE:] for b in range(B)],
                        axis=0)
    common = {"GF": GF, "CST": cst, "CBF": cbf, "ROPE": rope, "ROLE": role_last}
    launches = [list(range(depth))] if mode == "fused" else [[l] for l in range(depth)]
    cur = [np.ascontiguousarray(x[b], dtype=np.float32) for b in range(B)]
    for li, layer_ids in enumerate(launches):
        last = (li == len(launches) - 1)
        nc = _get_prog((n_tiles, len(layer_ids), last), n_tiles, list(range(len(layer_ids))), final_norm=last)
        base = dict(common)
        for slot, l in enumerate(layer_ids):
            base.update(_layer_inputs(slot, l, p))
        in_maps = []
        for b in range(B):
            m = dict(base)
            m["x"] = cur[b]
            in_maps.append(m)
        res = run_bass_kernel_spmd(nc, in_maps, core_ids=list(range(B)))
        cur = [np.asarray(res.results[b]["out"], dtype=np.float32) for b in range(B)]
    return np.stack(cur, axis=0)


MODE = "pipe"


def kernel(x, norm1_g, w_in, a_vnorm_g, a_ws, a_bs, b_w, b_scale, c_norm_g,
           w_out, norm2_g, w_up, conv_w, conv_b, w_down, final_g):
    p = dict(norm1_g=norm1_g, w_in=w_in, a_vnorm_g=a_vnorm_g, a_ws=a_ws, a_bs=a_bs, b_w=b_w, b_scale=b_scale,
             c_norm_g=c_norm_g, w_out=w_out, norm2_g=norm2_g, w_up=w_up, conv_w=conv_w, conv_b=conv_b,
             w_down=w_down, final_g=final_g)
    p = {k: np.asarray(v, dtype=np.float32) for k, v in p.items()}
    return run_model(np.asarray(x, dtype=np.float32), p, mode=MODE)
```
